# Optimizing a Trainium2 kernel written in Bass

```python
import math
import jax, jax.numpy as jnp
from jax import lax
import numpy as np

D_MODEL = 1024
BATCH = 4
SEQ = 4096
DEPTH = 2

D_INNER = 2 * D_MODEL
GROUP_WIDTH = D_INNER // 4
GRID_W = 64

ATTN_HEADS = 8
ATTN_KV_HEADS = 2
ATTN_HEAD_DIM = GROUP_WIDTH // ATTN_HEADS
Q_BLOCK = 128
ROPE_THETA = 10000.0

HGRN_HEADS = 4
HGRN_WIDTH = GROUP_WIDTH
HGRN_HEAD_V = HGRN_WIDTH // HGRN_HEADS
HGRN_EXPAND = 128
HGRN_KEY_WIDTH = HGRN_HEADS * HGRN_EXPAND

SSD_HEADS = 8
SSD_HEAD_DIM = GROUP_WIDTH // SSD_HEADS
SSD_WIDTH = SSD_HEADS * SSD_HEAD_DIM
SSD_GROUPS = 2
SSD_STATE = 128
SSD_CONV_WIDTH = 5
SSD_CONV_CH = SSD_WIDTH + 2 * SSD_GROUPS * SSD_STATE
SSD_CHUNK = 128

GLA_HEADS = 4
GLA_WIDTH = GROUP_WIDTH
GLA_KEY_WIDTH = GLA_WIDTH // 2
GLA_HEAD_K = GLA_KEY_WIDTH // GLA_HEADS
GLA_HEAD_V = GLA_WIDTH // GLA_HEADS
GLA_GATE_RANK = 16
GLA_GATE_NORMALIZER = 16.0

LIN_CHUNK = 64

DEEPNORM_ALPHA = (2 * DEPTH) ** 0.25
DEEPNORM_BETA = (8 * DEPTH) ** -0.25

IN_SPLITS = (
    ATTN_HEADS * ATTN_HEAD_DIM, ATTN_KV_HEADS * ATTN_HEAD_DIM, ATTN_KV_HEADS * ATTN_HEAD_DIM, GROUP_WIDTH,
    HGRN_KEY_WIDTH, HGRN_KEY_WIDTH, HGRN_KEY_WIDTH, HGRN_WIDTH, HGRN_WIDTH,
    SSD_CONV_CH, SSD_HEADS, SSD_HEADS, SSD_WIDTH,
    GLA_KEY_WIDTH, GLA_KEY_WIDTH, GLA_WIDTH, GLA_GATE_RANK, GLA_GATE_RANK, GLA_WIDTH,
)
N_IN = sum(IN_SPLITS)

kernel_name = 'hybrid_parallel_bidir_encoder'


def rms_norm(x, gain, eps=1e-6):
    xf = x.astype(jnp.float32)
    y = xf * lax.rsqrt(jnp.mean(xf * xf, axis=-1, keepdims=True) + eps)
    return (y * gain.astype(jnp.float32)).astype(x.dtype)


def layer_norm(x, gain, bias, eps=1e-5):
    xf = x.astype(jnp.float32)
    mu = jnp.mean(xf, axis=-1, keepdims=True)
    xc = xf - mu
    var = jnp.mean(xc * xc, axis=-1, keepdims=True)
    return (xc * lax.rsqrt(var + eps) * gain.astype(jnp.float32) + bias.astype(jnp.float32)).astype(x.dtype)


def split_columns(h, sizes):
    cuts = []
    acc = 0
    for w in sizes[:-1]:
        acc += w
        cuts.append(acc)
    return jnp.split(h, cuts, axis=-1)


def flip_seq(t):
    return jnp.flip(t, axis=1)


def axial_rope_tables(n_tokens):
    rows = n_tokens // GRID_W
    row_pos = jnp.repeat(jnp.arange(rows, dtype=jnp.float32), GRID_W)
    col_pos = jnp.tile(jnp.arange(GRID_W, dtype=jnp.float32), rows)
    axis_dim = ATTN_HEAD_DIM // 2
    inv_freq = jnp.power(ROPE_THETA, -jnp.arange(0, axis_dim, 2, dtype=jnp.float32) / axis_dim)
    ang_r = row_pos[:, None] * inv_freq
    ang_c = col_pos[:, None] * inv_freq
    return (jnp.cos(ang_r), jnp.sin(ang_r), jnp.cos(ang_c), jnp.sin(ang_c))


def _rotate(x, cos, sin):
    x1, x2 = jnp.split(x, 2, axis=-1)
    cos = cos[None, :, None, :].astype(x.dtype)
    sin = sin[None, :, None, :].astype(x.dtype)
    return jnp.concatenate([x1 * cos - x2 * sin, x2 * cos + x1 * sin], axis=-1)


def apply_axial_rope(x, rope):
    cos_r, sin_r, cos_c, sin_c = rope
    x_row, x_col = jnp.split(x, 2, axis=-1)
    return jnp.concatenate([_rotate(x_row, cos_r, sin_r), _rotate(x_col, cos_c, sin_c)], axis=-1)


def attention_branch(q_raw, k_raw, v_raw, z, q_gain, k_gain, rope):
    Bsz, L, _ = q_raw.shape
    group = ATTN_HEADS // ATTN_KV_HEADS
    q = rms_norm(q_raw.reshape(Bsz, L, ATTN_HEADS, ATTN_HEAD_DIM), q_gain)
    k = rms_norm(k_raw.reshape(Bsz, L, ATTN_KV_HEADS, ATTN_HEAD_DIM), k_gain)
    v = v_raw.reshape(Bsz, L, ATTN_KV_HEADS, ATTN_HEAD_DIM)
    q = apply_axial_rope(q, rope)
    k = apply_axial_rope(k, rope)
    q = q.reshape(Bsz, L // Q_BLOCK, Q_BLOCK, ATTN_KV_HEADS, group, ATTN_HEAD_DIM).transpose(1, 0, 2, 3, 4, 5)
    scale = ATTN_HEAD_DIM ** -0.5

    def attend(q_blk):
        s = jnp.einsum('bqkgd,bskd->bkgqs', q_blk, k).astype(jnp.float32) * scale
        p = jax.nn.softmax(s, axis=-1).astype(v.dtype)
        return jnp.einsum('bkgqs,bskd->bqkgd', p, v)

    o = lax.map(attend, q)
    o = o.transpose(1, 0, 2, 3, 4, 5).reshape(Bsz, L, ATTN_HEADS * ATTN_HEAD_DIM)
    return o * jax.nn.silu(z)


def chunked_gated_scan(q, k, v, log_g):
    Bsz, L, H, K = q.shape
    V = v.shape[-1]
    n_chunks = L // LIN_CHUNK
    f32 = jnp.float32

    def to_chunks(t):
        return t.astype(f32).reshape(Bsz, n_chunks, LIN_CHUNK, H, t.shape[-1]).transpose(1, 0, 3, 2, 4)

    qc, kc, vc, gc = to_chunks(q), to_chunks(k), to_chunks(v), to_chunks(log_g)
    lower = jnp.tril(jnp.ones((LIN_CHUNK, LIN_CHUNK), bool))[:, :, None]

    def step(state, blk):
        qi, ki, vi, gi = blk
        b = jnp.cumsum(gi, axis=2)
        rel = jnp.where(lower, b[:, :, :, None, :] - b[:, :, None, :, :], -jnp.inf)
        scores = jnp.einsum('bhtk,bhsk,bhtsk->bhts', qi, ki, jnp.exp(rel))
        out = (jnp.einsum('bhts,bhsv->bhtv', scores, vi)
               + jnp.einsum('bhtk,bhkv->bhtv', qi * jnp.exp(b), state))
        b_end = b[:, :, -1:, :]
        state = (jnp.exp(b_end[:, :, 0, :])[..., None] * state
                 + jnp.einsum('bhsk,bhsv->bhkv', ki * jnp.exp(b_end - b), vi))
        return state, out

    state0 = jnp.zeros((Bsz, H, K, V), f32)
    _, out = lax.scan(step, state0, (qc, kc, vc, gc))
    return out.transpose(1, 0, 3, 2, 4).reshape(Bsz, L, H, V).astype(v.dtype)


def bidirectional_gated_scan(q, k_fwd, log_g_fwd, k_bwd, log_g_bwd, v):
    o_f = chunked_gated_scan(q, k_fwd, v, log_g_fwd)
    o_b = flip_seq(chunked_gated_scan(flip_seq(q), flip_seq(k_bwd), flip_seq(v), flip_seq(log_g_bwd)))
    return o_f + o_b


def hgrn2_branch(q_raw, f_fwd_raw, f_bwd_raw, i_raw, z, lower_bound, norm_gain):
    Bsz, L, _ = q_raw.shape
    key_shape = (Bsz, L, HGRN_HEADS, HGRN_EXPAND)
    q = jax.nn.silu(q_raw).reshape(key_shape) * (HGRN_EXPAND ** -0.5)
    v = i_raw.reshape(Bsz, L, HGRN_HEADS, HGRN_HEAD_V)
    lb = jnp.maximum(lower_bound.astype(jnp.float32), 0.0).reshape(HGRN_HEADS, HGRN_EXPAND)
    log_lb = jnp.log(lb)
    log_1m_lb = jnp.log1p(-lb)

    def forget(f_raw):
        f = f_raw.astype(jnp.float32).reshape(key_shape)
        log_f = jnp.logaddexp(log_lb, log_1m_lb + jax.nn.log_sigmoid(f))
        one_minus_f = jnp.exp(log_1m_lb + jax.nn.log_sigmoid(-f))
        return one_minus_f, log_f

    k_f, g_f = forget(f_fwd_raw)
    k_b, g_b = forget(f_bwd_raw)
    o = bidirectional_gated_scan(q, k_f, g_f, k_b, g_b, v).reshape(Bsz, L, HGRN_WIDTH)
    return rms_norm(o, norm_gain) * jax.nn.silu(z)


def segsum(a):
    T = a.shape[-1]
    ae = jnp.broadcast_to(a[..., None], a.shape + (T,))
    cs = jnp.cumsum(jnp.where(jnp.tril(jnp.ones((T, T), bool), -1), ae, 0.0), axis=-2)
    return jnp.where(jnp.tril(jnp.ones((T, T), bool)), cs, -jnp.inf)


def ssd_scan(x, dt, a_coef, b_in, c_in):
    Bsz, L, H, P = x.shape
    G, N = b_in.shape[-2], b_in.shape[-1]
    R = H // G
    C = SSD_CHUNK
    nc = L // C
    f32 = jnp.float32
    xdt = (x.astype(f32) * dt[..., None]).reshape(Bsz, nc, C, G, R, P)
    a = (dt * a_coef).reshape(Bsz, nc, C, H).transpose(0, 1, 3, 2)
    a_cs = jnp.cumsum(a, axis=-1)
    bc = b_in.astype(f32).reshape(Bsz, nc, C, G, N)
    cc = c_in.astype(f32).reshape(Bsz, nc, C, G, N)
    decay_in = jnp.exp(segsum(a)).reshape(Bsz, nc, G, R, C, C)
    cb = jnp.einsum('bclgn,bcsgn->bcgls', cc, bc)
    y_diag = jnp.einsum('bcgls,bcgrls,bcsgrp->bclgrp', cb, decay_in, xdt)
    decay_to_end = jnp.exp(a_cs[..., -1:] - a_cs).reshape(Bsz, nc, G, R, C)
    states = jnp.einsum('bclgn,bcgrl,bclgrp->bcgrpn', bc, decay_to_end, xdt)
    states = jnp.concatenate([jnp.zeros_like(states[:, :1]), states], axis=1)
    chunk_a = jnp.pad(a_cs[..., -1].transpose(0, 2, 1), ((0, 0), (0, 0), (1, 0)))
    decay_chunk = jnp.exp(segsum(chunk_a)).reshape(Bsz, G, R, nc + 1, nc + 1)
    states = jnp.einsum('bgrzc,bcgrpn->bzgrpn', decay_chunk, states)[:, :-1]
    y_off = jnp.einsum('bclgn,bcgrpn,bcgrl->bclgrp', cc, states, jnp.exp(a_cs).reshape(Bsz, nc, G, R, C))
    return (y_diag + y_off).reshape(Bsz, L, H, P)


def centred_depthwise_conv(u, w, b):
    width, ch = w.shape
    y = lax.conv_general_dilated(u, w[:, None, :].astype(u.dtype), window_strides=(1,),
                                 padding=[((width - 1) // 2, width // 2)],
                                 dimension_numbers=('NWC', 'WIO', 'NWC'),
                                 feature_group_count=ch)
    return y + b.astype(u.dtype)


def ssd_branch(xbc, dt_fwd_raw, dt_bwd_raw, z, conv_w, conv_b, dt_bias, a_log, d_skip, norm_gain):
    Bsz, L, _ = xbc.shape
    f32 = jnp.float32
    u = jax.nn.silu(centred_depthwise_conv(xbc, conv_w, conv_b))
    xs, b_in, c_in = jnp.split(u, [SSD_WIDTH, SSD_WIDTH + SSD_GROUPS * SSD_STATE], axis=-1)
    xs = xs.reshape(Bsz, L, SSD_HEADS, SSD_HEAD_DIM)
    b_in = b_in.reshape(Bsz, L, SSD_GROUPS, SSD_STATE)
    c_in = c_in.reshape(Bsz, L, SSD_GROUPS, SSD_STATE)
    dt_f = jax.nn.softplus(dt_fwd_raw.astype(f32) + dt_bias[0].astype(f32))
    dt_b = jax.nn.softplus(dt_bwd_raw.astype(f32) + dt_bias[1].astype(f32))
    a_f = -jnp.exp(a_log[0].astype(f32))
    a_b = -jnp.exp(a_log[1].astype(f32))
    y_f = ssd_scan(xs, dt_f, a_f, b_in, c_in)
    y_b = flip_seq(ssd_scan(flip_seq(xs), flip_seq(dt_b), a_b, flip_seq(b_in), flip_seq(c_in)))
    y = y_f + y_b + d_skip.astype(f32)[:, None] * xs.astype(f32)
    y = y.reshape(Bsz, L, SSD_WIDTH).astype(z.dtype)
    return rms_norm(y * jax.nn.silu(z), norm_gain)


def gla_branch(q_raw, k_raw, v_raw, gk_fwd_low, gk_bwd_low, z, gk_w2, gk_b, norm_gain):
    Bsz, L, _ = q_raw.shape
    key_shape = (Bsz, L, GLA_HEADS, GLA_HEAD_K)
    q = q_raw.reshape(key_shape) * (GLA_HEAD_K ** -0.5)
    k = k_raw.reshape(key_shape)
    v = v_raw.reshape(Bsz, L, GLA_HEADS, GLA_HEAD_V)

    def log_gate(low, w2, b2):
        gk = jnp.einsum('bsr,rk->bsk', low, w2) + b2
        return (jax.nn.log_sigmoid(gk.astype(jnp.float32)) / GLA_GATE_NORMALIZER).reshape(key_shape)

    g_f = log_gate(gk_fwd_low, gk_w2[0], gk_b[0])
    g_b = log_gate(gk_bwd_low, gk_w2[1], gk_b[1])
    o = bidirectional_gated_scan(q, k, g_f, k, g_b, v)
    o = rms_norm(o, norm_gain).reshape(Bsz, L, GLA_WIDTH)
    return o * jax.nn.silu(z)


def setup_inputs(seed: int = 0) -> dict:
    key = jax.random.key(seed)
    ks = jax.random.split(key, 20)
    f32 = jnp.float32

    def nrm(k, shape, scale):
        return jax.random.normal(k, shape, f32) * scale

    x = nrm(ks[0], (BATCH, SEQ, D_MODEL), 1.0)
    w_in = nrm(ks[1], (DEPTH, D_MODEL, N_IN), D_MODEL ** -0.5)
    attn_q_norm = 1.0 + nrm(ks[2], (DEPTH, ATTN_HEAD_DIM), 0.02)
    attn_k_norm = 1.0 + nrm(ks[3], (DEPTH, ATTN_HEAD_DIM), 0.02)
    hgrn_lb_logits = nrm(ks[4], (DEPTH, HGRN_KEY_WIDTH), 0.1)
    hgrn_norm = 1.0 + nrm(ks[5], (DEPTH, HGRN_WIDTH), 0.02)
    ssd_conv_w = nrm(ks[6], (DEPTH, SSD_CONV_WIDTH, SSD_CONV_CH), SSD_CONV_WIDTH ** -0.5)
    ssd_conv_b = nrm(ks[7], (DEPTH, SSD_CONV_CH), 0.02)
    dt0 = jnp.exp(jax.random.uniform(ks[8], (DEPTH, 2, SSD_HEADS), f32,
                                     minval=math.log(1e-3), maxval=math.log(1e-1)))
    ssd_dt_bias = dt0 + jnp.log(-jnp.expm1(-dt0))
    ssd_a_log = jnp.log(jax.random.uniform(ks[9], (DEPTH, 2, SSD_HEADS), f32, minval=1.0, maxval=16.0))
    ssd_d = 1.0 + nrm(ks[10], (DEPTH, SSD_HEADS), 0.02)
    ssd_norm = 1.0 + nrm(ks[11], (DEPTH, SSD_WIDTH), 0.02)
    gla_gk_w2 = nrm(ks[12], (DEPTH, 2, GLA_GATE_RANK, GLA_KEY_WIDTH), GLA_GATE_RANK ** -0.5)
    gla_gk_b = nrm(ks[13], (DEPTH, 2, GLA_KEY_WIDTH), 0.02)
    gla_norm = 1.0 + nrm(ks[14], (DEPTH, GLA_HEAD_V), 0.02)
    w_out = nrm(ks[15], (DEPTH, D_INNER, D_MODEL), (D_INNER ** -0.5) * DEEPNORM_BETA)
    ln_g = 1.0 + nrm(ks[16], (DEPTH, D_MODEL), 0.02)
    ln_b = nrm(ks[17], (DEPTH, D_MODEL), 0.02)
    return {'x': x, 'w_in': w_in, 'attn_q_norm': attn_q_norm, 'attn_k_norm': attn_k_norm,
            'hgrn_lb_logits': hgrn_lb_logits, 'hgrn_norm': hgrn_norm,
            'ssd_conv_w': ssd_conv_w, 'ssd_conv_b': ssd_conv_b, 'ssd_dt_bias': ssd_dt_bias,
            'ssd_a_log': ssd_a_log, 'ssd_d': ssd_d, 'ssd_norm': ssd_norm,
            'gla_gk_w2': gla_gk_w2, 'gla_gk_b': gla_gk_b, 'gla_norm': gla_norm,
            'w_out': w_out, 'ln_g': ln_g, 'ln_b': ln_b}


def reference(x, w_in, attn_q_norm, attn_k_norm, hgrn_lb_logits, hgrn_norm,
              ssd_conv_w, ssd_conv_b, ssd_dt_bias, ssd_a_log, ssd_d, ssd_norm,
              gla_gk_w2, gla_gk_b, gla_norm, w_out, ln_g, ln_b):
    L = x.shape[1]
    rope = axial_rope_tables(L)
    lower_bounds = jnp.cumsum(jax.nn.softmax(hgrn_lb_logits.astype(jnp.float32), axis=0), axis=0)
    lower_bounds = lower_bounds - lower_bounds[0]
    for i in range(DEPTH):
        h = jnp.einsum('bsd,dn->bsn', x, w_in[i])
        (a_q, a_k, a_v, a_z,
         h_q, h_ff, h_fb, h_i, h_z,
         s_xbc, s_dtf, s_dtb, s_z,
         g_q, g_k, g_v, g_lf, g_lb, g_z) = split_columns(h, IN_SPLITS)
        y_a = attention_branch(a_q, a_k, a_v, a_z, attn_q_norm[i], attn_k_norm[i], rope)
        y_h = hgrn2_branch(h_q, h_ff, h_fb, h_i, h_z, lower_bounds[i], hgrn_norm[i])
        y_s = ssd_branch(s_xbc, s_dtf, s_dtb, s_z, ssd_conv_w[i], ssd_conv_b[i],
                         ssd_dt_bias[i], ssd_a_log[i], ssd_d[i], ssd_norm[i])
        y_g = gla_branch(g_q, g_k, g_v, g_lf, g_lb, g_z, gla_gk_w2[i], gla_gk_b[i], gla_norm[i])
        mixed = jnp.concatenate([y_a.astype(x.dtype), y_h.astype(x.dtype),
                                 y_s.astype(x.dtype), y_g.astype(x.dtype)], axis=-1)
        out = jnp.einsum('bsn,nd->bsd', mixed, w_out[i])
        x = layer_norm(DEEPNORM_ALPHA * x + out, ln_g[i], ln_b[i])
    return x
```

```python
import contextlib
import numpy as np
import concourse.bass as bass
import concourse.mybir as mybir
from concourse.bass_utils import run_bass_kernel_spmd

F32 = mybir.dt.float32
BF16 = mybir.dt.bfloat16
AF = mybir.ActivationFunctionType
ALU = mybir.AluOpType
AX = mybir.AxisListType

ENGS = ("pe", "act", "dve", "pool", "sp")
SAME_ENGINE_SYNC = True

D = 1024
NIN = 6960
DEPTH = 2
ALPHA = float((2 * DEPTH) ** 0.25)
A_Q, A_K, A_V, A_Z = 0, 512, 640, 768
H_Q, H_FF, H_FB, H_I, H_Z = 1280, 1792, 2304, 2816, 3328
S_XBC, S_DTF, S_DTB, S_Z = 3840, 4864, 4872, 4880
G_Q, G_K, G_V, G_LF, G_LB, G_Z = 5392, 5648, 5904, 6416, 6432, 6448
NEG = -1.0e5


class Buf:
    def __init__(self, name, t=None):
        self.name = name
        self.t = t
        self.w = []
        self.r = []
        self.dsem = {}

    def __getitem__(self, k):
        return self.t[k]


class K:
    def __init__(self, nc):
        self.nc = nc
        self.es = contextlib.ExitStack()
        self.ph = None
        self.prog = {e: [] for e in ENGS}
        self.cnt = {}
        self.sem = {}
        self.waited = {e: {} for e in ENGS}
        self.free_dsems = {}
        self.ph_bufs = []
        self.uid = 0
        for e in ("pe", "act", "dve", "pool"):
            self._mksem("c_" + e)
        self.n_instr = 0

    def _mksem(self, name):
        s = self.es.enter_context(self.nc.semaphore(name))
        self.sem[name] = s
        self.cnt[name] = 0
        return name

    def begin(self):
        self.ph = contextlib.ExitStack()
        self.ph_bufs = []

    def end(self):
        self.barrier()
        self.emit_block()
        for b in self.ph_bufs:
            for q_, s_ in b.dsem.items():
                self.free_dsems.setdefault(q_, []).append(s_)
            b.dsem = {}
        self.ph.close()
        self.ph = None

    def barrier(self):
        for e in ENGS:
            waits = []
            for s, c in self.cnt.items():
                if c > 0 and self.waited[e].get(s, 0) < c:
                    if s == "c_" + e:
                        continue
                    waits.append((s, c))
                    self.waited[e][s] = c
            if waits:
                self.prog[e].append((waits, None, None))

    def sb(self, name, shape, dtype=F32):
        self.uid += 1
        t = self.ph.enter_context(self.nc.sbuf_tensor("%s_%d" % (name, self.uid), list(shape), dtype))
        b = Buf(name, t)
        self.ph_bufs.append(b)
        return b

    def ps(self, name, shape, dtype=F32):
        self.uid += 1
        t = self.ph.enter_context(self.nc.psum_tensor("%s_%d" % (name, self.uid), list(shape), dtype))
        b = Buf(name, t)
        self.ph_bufs.append(b)
        return b

    def view(self, base, name):
        b = Buf(name, base.t)
        self.ph_bufs.append(b)
        return b

    def _deps(self, eng, reads, writes):
        evs = []
        for b in reads:
            evs.extend(b.w)
        for b in writes:
            evs.extend(b.w)
            evs.extend(b.r)
        need = {}
        for (s, v, e) in evs:
            if e == eng and (eng == "pe" or (not SAME_ENGINE_SYNC and eng in ("act", "dve", "pool"))):
                continue
            if need.get(s, 0) < v:
                need[s] = v
        out = []
        wd = self.waited[eng]
        for s, v in need.items():
            if wd.get(s, 0) >= v:
                continue
            wd[s] = v
            out.append((s, v))
        return out

    def _record(self, ev, reads, writes):
        for b in writes:
            b.w = [ev]
            b.r = []
        for b in reads:
            if b not in writes:
                b.r.append(ev)
                if len(b.r) > 48:
                    m = {}
                    for (s, v, e) in b.r:
                        if s not in m or m[s][1] < v:
                            m[s] = (s, v, e)
                    b.r = list(m.values())

    def op(self, eng, fn, reads=(), writes=()):
        waits = self._deps(eng, reads, writes)
        s = "c_" + eng
        self.cnt[s] += 1
        ev = (s, self.cnt[s], eng)
        self.prog[eng].append((waits, fn, (s, 1)))
        self._record(ev, reads, writes)
        self.n_instr += 1
        return ev

    def dma(self, q, out_ap, in_ap, sbuf, reads=(), writes=()):
        if q not in sbuf.dsem:
            fl = self.free_dsems.get(q, [])
            if fl:
                sbuf.dsem[q] = fl.pop()
            else:
                sbuf.dsem[q] = self._mksem("d%s_%d" % (q, len(self.sem)))
        s = sbuf.dsem[q]
        waits = self._deps(q, reads, writes)
        self.cnt[s] += 16
        ev = (s, self.cnt[s], "dma")
        self.prog[q].append((waits, lambda e: e.dma_start(out=out_ap, in_=in_ap), (s, 16)))
        self._record(ev, reads, writes)
        self.n_instr += 1
        return ev

    def coll(self, kind, groups, in_ap, out_ap, reads=(), writes=()):
        if "coll" not in self.sem:
            self._mksem("coll")
        s = "coll"
        waits = self._deps("pool", reads, writes)
        self.cnt[s] += 16
        ev = (s, self.cnt[s], "dma")
        self.prog["pool"].append((waits, lambda e: e.collective_compute(
            kind, ALU.bypass, replica_groups=groups, ins=[in_ap], outs=[out_ap]), (s, 16)))
        self._record(ev, reads, writes)
        self.n_instr += 1
        return ev

    def emit_block(self):
        nc = self.nc
        sem = self.sem
        prog = self.prog

        def run(engname):
            def body(eng):
                for (waits, fn, inc) in prog[engname]:
                    for (s, v) in waits:
                        eng.wait_ge(sem[s], v)
                    if fn is not None:
                        ins = fn(eng)
                        ins.then_inc(sem[inc[0]], inc[1])
            return body

        with nc.Block() as block:
            block.tensor(run("pe"))
            block.scalar(run("act"))
            block.vector(run("dve"))
            block.gpsimd(run("pool"))
            block.sync(run("sp"))
        self.prog = {e: [] for e in ENGS}

    def emit(self):
        self.es.close()


CST = {}
_o = 0
for _n, _w in (("ident", 128), ("ones", 128), ("trib_f", 128), ("trib_b", 128), ("sel_f", 4), ("sel_b", 4),
               ("rowm", 4), ("tri_f", 128), ("tri_b", 128), ("neg_f", 512), ("neg_b", 512)):
    CST[_n] = (_o, _w)
    _o += _w
NCST = _o

PRM = {}
_o = 0
for _n, _w in (("aqn", 64), ("akn", 64), ("lbl", 1024), ("hnorm", 4), ("convw", 40), ("convb", 8), ("dtb", 16),
               ("alog", 16), ("dsk", 8), ("snorm", 512), ("w2", 512), ("gkb", 512), ("gnorm", 4), ("lng", 1024),
               ("lnb", 1024)):
    PRM[_n] = (_o, _w)
    _o += _w
NPRM = _o


def make_consts(L):
    c = np.zeros((128, NCST), np.float32)
    s = np.arange(128)[:, None]
    t = np.arange(128)[None, :]
    same = (s // 32) == (t // 32)

    def put(n, a):
        o, w = CST[n]
        c[:, o:o + w] = a
    put("ident", np.eye(128))
    put("ones", np.ones((128, 128)))
    put("trib_f", (same & (s <= t)).astype(np.float32))
    put("trib_b", (same & (s >= t)).astype(np.float32))
    sf = np.zeros((128, 4)); sb_ = np.zeros((128, 4)); rm = np.zeros((128, 4))
    for u in range(4):
        sf[32 * u + 31, u] = 1
        sb_[32 * u, u] = 1
        rm[32 * u:32 * u + 32, u] = 1
    put("sel_f", sf); put("sel_b", sb_); put("rowm", rm)
    put("tri_f", (s <= t).astype(np.float32))
    put("tri_b", (s >= t).astype(np.float32))
    put("neg_f", np.tile(np.where(s <= t, 0.0, NEG), (1, 4)))
    put("neg_b", np.tile(np.where(s >= t, 0.0, NEG), (1, 4)))
    pos = np.arange(L)
    row = (pos // 64).astype(np.float32)
    col = (pos % 64).astype(np.float32)
    inv = np.power(np.float32(10000.0), -np.arange(0, 32, 2, dtype=np.float32) / np.float32(32)).astype(np.float32)
    ar = row[:, None] * inv
    ac = col[:, None] * inv
    cos = np.concatenate([np.cos(ar), np.cos(ar), np.cos(ac), np.cos(ac)], 1).astype(np.float32)
    sin = np.concatenate([np.sin(ar), np.sin(ar), np.sin(ac), np.sin(ac)], 1).astype(np.float32)
    return c, np.ascontiguousarray(np.concatenate([cos, sin], 1))


def make_params(inp):
    p = np.zeros((DEPTH, 128, NPRM), np.float32)

    def put(l, n, a):
        o, w = PRM[n]
        p[l, :, o:o + w] = a
    for l in range(DEPTH):
        put(l, "aqn", np.broadcast_to(inp["attn_q_norm"][l][None, :], (128, 64)))
        put(l, "akn", np.broadcast_to(inp["attn_k_norm"][l][None, :], (128, 64)))
        put(l, "lbl", np.broadcast_to(inp["hgrn_lb_logits"].reshape(1, 1024), (128, 1024)))
        put(l, "hnorm", inp["hgrn_norm"][l].reshape(4, 128).T)
        put(l, "convw", inp["ssd_conv_w"][l].reshape(5, 8, 128).transpose(2, 1, 0).reshape(128, 40))
        put(l, "convb", inp["ssd_conv_b"][l].reshape(8, 128).T)
        put(l, "dtb", np.broadcast_to(inp["ssd_dt_bias"][l].reshape(1, 16), (128, 16)))
        put(l, "alog", np.broadcast_to(inp["ssd_a_log"][l].reshape(1, 16), (128, 16)))
        put(l, "dsk", np.broadcast_to(inp["ssd_d"][l].reshape(1, 8), (128, 8)))
        put(l, "snorm", np.broadcast_to(inp["ssd_norm"][l].reshape(1, 512), (128, 512)))
        w2 = np.zeros((128, 512), np.float32)
        w2[:16] = inp["gla_gk_w2"][l].transpose(1, 0, 2).reshape(16, 512)
        put(l, "w2", w2)
        put(l, "gkb", np.broadcast_to(inp["gla_gk_b"][l].reshape(1, 512), (128, 512)))
        put(l, "gnorm", np.repeat(inp["gla_norm"][l].reshape(128, 1), 4, axis=1))
        put(l, "lng", np.broadcast_to(inp["ln_g"][l][None, :], (128, 1024)))
        put(l, "lnb", np.broadcast_to(inp["ln_b"][l][None, :], (128, 1024)))
    return p


class G:
    pass


def _alt(i):
    return "act" if i % 2 == 0 else "dve"


def copy_op(k, eng, out_ap, in_ap, reads, writes):
    if eng == "act":
        k.op("act", lambda e: e.copy(out=out_ap, in_=in_ap), reads, writes)
    else:
        k.op(eng, lambda e: e.tensor_copy(out=out_ap, in_=in_ap), reads, writes)


def load_cst(k, g, names):
    out = {}
    for n in names:
        o, w = CST[n]
        b = k.sb("c_" + n, [128, w])
        k.dma("sp", b[:], g.cst[:, o:o + w], b, writes=[b])
        out[n] = b
    return out


def load_prm(k, g, l, names):
    out = {}
    for n in names:
        o, w = PRM[n]
        b = k.sb("p_" + n, [128, w])
        k.dma("sp", b[:], g.prm[l, :, o:o + w], b, writes=[b])
        out[n] = b
    return out


def rstd_op(k, out_b, out_ap, in_b, in_ap, scale, eps):
    k.op("dve", lambda e: e.tensor_scalar(out=out_ap, in0=in_ap, scalar1=scale, scalar2=eps, op0=ALU.mult, op1=ALU.add),
         [in_b], [out_b])
    k.op("act", lambda e: e.activation(out=out_ap, in_=out_ap, func=AF.Ln), [out_b], [out_b])
    k.op("act", lambda e: e.activation(out=out_ap, in_=out_ap, func=AF.Exp, scale=-0.5), [out_b], [out_b])


def silu_op(k, out_b, out_ap, in_b, in_ap, tmp_b, tmp_ap):
    k.op("act", lambda e: e.activation(out=tmp_ap, in_=in_ap, func=AF.Exp, scale=-1.0), [in_b], [tmp_b])
    k.op("act", lambda e: e.activation(out=tmp_ap, in_=tmp_ap, func=AF.Ln, bias=1.0, scale=1.0), [tmp_b], [tmp_b])
    k.op("act", lambda e: e.activation(out=tmp_ap, in_=tmp_ap, func=AF.Exp, scale=-1.0), [tmp_b], [tmp_b])
    k.op("dve", lambda e: e.tensor_tensor(out=out_ap, in0=in_ap, in1=tmp_ap, op=ALU.mult), [in_b, tmp_b], [out_b])


def phase_proj(k, g, l, x_ap):
    L, NT = g.L, g.NT
    k.begin()
    c = load_cst(k, g, ["ident"])
    ident = c["ident"]
    xT = k.sb("xT", [128, 8, L], BF16)
    xTb = [k.view(xT, "xT%d" % i) for i in range(NT)]
    xs = [k.sb("xs%d" % i, [128, 1024]) for i in range(2)]
    psT = [k.ps("psT%d" % i, [128, 512]) for i in range(2)]
    for i in range(NT):
        xb = xs[i % 2]
        k.dma("sp", xb[:], x_ap[i * 128:(i + 1) * 128, :], xb, writes=[xb])
        for hf in range(2):
            p = psT[hf]
            for cc in range(4):
                k.op("pe", lambda e, p=p, cc=cc, hf=hf, xb=xb: e.transpose(
                    out=p[:, cc * 128:(cc + 1) * 128], in_=xb[:, (hf * 4 + cc) * 128:(hf * 4 + cc + 1) * 128],
                    identity=ident[:]), [xb, ident], [p])
            copy_op(k, _alt(hf), xT[:, hf * 4:(hf + 1) * 4, i * 128:(i + 1) * 128],
                    p[:].rearrange("p (c t) -> p c t", c=4), [p], [xTb[i]])
    blocks = [(0, 512, 0), (512, 512, 0), (1024, 256, 0)]
    blocks += [(1280 + 512 * j, 512, 0) for j in range(5)]
    blocks += [(3840, 512, 1), (4352, 512, 1), (4864, 16, 0), (4880, 512, 0)]
    blocks += [(5392, 512, 0), (5904, 512, 0), (6416, 32, 0), (6448, 512, 0)]
    Wf = [k.sb("Wf%d" % i, [128, 8, 512]) for i in range(2)]
    Wb = [k.sb("Wb%d" % i, [128, 8, 512], BF16) for i in range(2)]
    stg = [k.sb("stg%d" % i, [128, 512]) for i in range(3)]
    psM = [k.ps("psM%d" % i, [128, 512]) for i in range(4)]
    wv = g.w_in[l].rearrange("(c p) n -> p c n", p=128)

    def loadw(bi):
        c0, wd, _ = blocks[bi]
        wf = Wf[bi % 2]
        k.dma("sp", wf[:, :, 0:wd], wv[:, :, c0:c0 + wd], wf, writes=[wf])

    def castw(bi):
        c0, wd, _ = blocks[bi]
        wf, wb = Wf[bi % 2], Wb[bi % 2]
        k.op("dve", lambda e: e.tensor_copy(out=wb[:, :, 0:wd], in_=wf[:, :, 0:wd]), [wf], [wb])

    loadw(0)
    castw(0)
    n = 0
    for bi, (c0, wd, mode) in enumerate(blocks):
        wb = Wb[bi % 2]
        if bi + 1 < len(blocks):
            loadw(bi + 1)
        if mode == 0:
            units = [(i,) for i in range(NT)]
        else:
            units = [(cc, tb) for cc in range(wd // 128) for tb in range(L // 512)]
        for ui, u in enumerate(units):
            if ui == len(units) // 2 and bi + 1 < len(blocks):
                castw(bi + 1)
            p = psM[n % 4]
            s = stg[n % 3]
            if mode == 0:
                i = u[0]
                for cc in range(8):
                    k.op("pe", lambda e, p=p, cc=cc, i=i, wb=wb, wd=wd: e.matmul(
                        p[:, 0:wd], lhsT=xT[:, cc, i * 128:(i + 1) * 128], rhs=wb[:, cc, 0:wd],
                        start=(cc == 0), stop=(cc == 7)), [xTb[i], wb], [p])
                copy_op(k, _alt(n), s[:, 0:wd], p[:, 0:wd], [p], [s])
                k.dma("pool", g.H[i * 128:(i + 1) * 128, c0:c0 + wd], s[:, 0:wd], s, reads=[s])
            else:
                cc2, tb = u
                for cc in range(8):
                    k.op("pe", lambda e, p=p, cc=cc, cc2=cc2, tb=tb, wb=wb: e.matmul(
                        p[:, 0:512], lhsT=wb[:, cc, cc2 * 128:(cc2 + 1) * 128], rhs=xT[:, cc, tb * 512:(tb + 1) * 512],
                        start=(cc == 0), stop=(cc == 7)), [xTb[tb * 4 + j] for j in range(4)] + [wb], [p])
                copy_op(k, _alt(n), s[:, 0:512], p[:, 0:512], [p], [s])
                r0 = c0 - S_XBC + cc2 * 128
                k.dma("pool", g.XBCT[r0:r0 + 128, tb * 512:(tb + 1) * 512], s[:, 0:512], s, reads=[s])
            n += 1
    k.end()


def norm_rope(k, nh, src, dst, gain, rope, sq, ss, tC, tS):
    W = nh * 64
    s3 = src[:, 0:W].rearrange("p (h d) -> p h d", h=nh)
    k.op("dve", lambda e: e.tensor_tensor(out=sq[:, 0:W], in0=src[:, 0:W], in1=src[:, 0:W], op=ALU.mult), [src], [sq])
    k.op("dve", lambda e: e.tensor_reduce(out=ss[:, 0:nh], in_=sq[:, 0:W].rearrange("p (h d) -> p h d", h=nh),
                                          axis=AX.X, op=ALU.add), [sq], [ss])
    rstd_op(k, ss, ss[:, 0:nh], ss, ss[:, 0:nh], 1.0 / 64, 1e-6)
    q3 = sq[:, 0:W].rearrange("p (h d) -> p h d", h=nh)
    k.op("dve", lambda e: e.tensor_tensor(out=q3, in0=s3, in1=ss[:, 0:nh].unsqueeze(2).to_broadcast([128, nh, 64]),
                                          op=ALU.mult), [src, ss], [sq])
    k.op("dve", lambda e: e.tensor_tensor(out=q3, in0=q3, in1=gain[:, 0:64].unsqueeze(1).to_broadcast([128, nh, 64]),
                                          op=ALU.mult), [sq, gain], [sq])
    k.op("dve", lambda e: e.tensor_tensor(out=tC[:, 0:W].rearrange("p (h d) -> p h d", h=nh), in0=q3,
                                          in1=rope[:, 0:64].unsqueeze(1).to_broadcast([128, nh, 64]), op=ALU.mult),
         [sq, rope], [tC])
    k.op("dve", lambda e: e.tensor_tensor(out=tS[:, 0:W].rearrange("p (h d) -> p h d", h=nh), in0=q3,
                                          in1=rope[:, 64:128].unsqueeze(1).to_broadcast([128, nh, 64]), op=ALU.mult),
         [sq, rope], [tS])
    o4 = dst[:, 0:W].rearrange("p (a b d) -> p a b d", b=2, d=16)
    c4 = tC[:, 0:W].rearrange("p (a b d) -> p a b d", b=2, d=16)
    s4 = tS[:, 0:W].rearrange("p (a b d) -> p a b d", b=2, d=16)
    k.op("dve", lambda e: e.tensor_tensor(out=o4[:, :, 0, :], in0=c4[:, :, 0, :], in1=s4[:, :, 1, :], op=ALU.subtract),
         [tC, tS], [dst])
    k.op("dve", lambda e: e.tensor_tensor(out=o4[:, :, 1, :], in0=c4[:, :, 1, :], in1=s4[:, :, 0, :], op=ALU.add),
         [tC, tS], [dst])


def phase_attn(k, g, l):
    L, NT = g.L, g.NT
    NP = NT // 2
    k.begin()
    c = load_cst(k, g, ["ident"])
    ident = c["ident"]
    pr = load_prm(k, g, l, ["aqn", "akn"])
    kT2 = k.sb("kT2", [128, 2, NP * 128], BF16)
    kTb = [k.view(kT2, "kT%d" % i) for i in range(NT)]
    va = k.sb("va", [128, NT, 2, 128], BF16)
    vab = [k.view(va, "va%d" % i) for i in range(NT)]
    k.op("pool", lambda e: e.memset(va[:], 1.0), [], vab)
    kv = [k.sb("kv%d" % i, [128, 256]) for i in range(2)]
    rp = [k.sb("rp%d" % i, [128, 128]) for i in range(2)]
    sq = k.sb("sq", [128, 512]); ss = k.sb("ss", [128, 8]); tC = k.sb("tC", [128, 512]); tS = k.sb("tS", [128, 512])
    kr = k.sb("kr", [128, 128]); krp = k.sb("krp", [128, 2, 128])
    k.op("pool", lambda e: e.memset(krp[:], 0.0), [], [krp])
    psK = k.ps("psK", [128, 512])
    for i in range(NT):
        kvb, rpb = kv[i % 2], rp[i % 2]
        p_, j = i % 2, i // 2
        k.dma("sp", kvb[:], g.H[i * 128:(i + 1) * 128, A_K:A_K + 256], kvb, writes=[kvb])
        k.dma("sp", rpb[:], g.rope[i * 128:(i + 1) * 128, :], rpb, writes=[rpb])
        norm_rope(k, 2, kvb, kr, pr["akn"], rpb, sq, ss, tC, tS)
        k.op("dve", lambda e, p_=p_: e.tensor_copy(out=krp[:, :, p_ * 64:(p_ + 1) * 64],
                                                   in_=kr[:, 0:128].rearrange("p (g d) -> p g d", g=2)), [kr], [krp])
        for h in range(2):
            k.op("pe", lambda e, h=h: e.transpose(out=psK[:, h * 128:(h + 1) * 128], in_=krp[:, h, :],
                                                  identity=ident[:]), [krp, ident], [psK])
        copy_op(k, "act", kT2[p_ * 64:(p_ + 1) * 64, :, j * 128:(j + 1) * 128],
                psK[p_ * 64:(p_ + 1) * 64, 0:256].rearrange("p (h t) -> p h t", h=2), [psK], [kTb[i]])
        k.op("dve", lambda e, i=i, kvb=kvb: e.tensor_copy(out=va[:, i, :, 0:64],
                                                          in_=kvb[:, 128:256].rearrange("p (h d) -> p h d", h=2)),
             [kvb], [vab[i]])
    qz = [k.sb("qz%d" % i, [128, 1024]) for i in range(2)]
    qr = k.sb("qr", [128, 512])
    qr2 = [k.sb("qr2_%d" % i, [128, 8, 2, 64]) for i in range(2)]
    qT = [k.sb("qT%d" % i, [128, 8, 128], BF16) for i in range(2)]
    sz = [k.sb("sz%d" % i, [128, 512]) for i in range(3)]
    pt = [k.sb("pt%d" % i, [128, 512], BF16) for i in range(6)]
    oT = [k.sb("oT%d" % i, [128, 512]) for i in range(2)]
    rec = k.sb("rec", [128, 8]); ob = k.sb("ob", [128, 512]); yb = k.sb("yb", [128, 4, 128], BF16)
    szt = k.sb("szt", [128, 512])
    psQ = k.ps("psQ", [128, 512])
    psS = [k.ps("psS%d" % i, [128, 512]) for i in range(4)]
    psO = [k.ps("psO%d" % i, [128, 512]) for i in range(2)]

    def pre_dve(i):
        qb, rpb = qz[i % 2], rp[i % 2]
        k.dma("sp", qb[:, 0:512], g.H[i * 128:(i + 1) * 128, A_Q:A_Q + 512], qb, writes=[qb])
        k.dma("sp", qb[:, 512:1024], g.H[i * 128:(i + 1) * 128, A_Z:A_Z + 512], qb, writes=[qb])
        k.dma("sp", rpb[:], g.rope[i * 128:(i + 1) * 128, :], rpb, writes=[rpb])
        norm_rope(k, 8, qb, qr, pr["aqn"], rpb, sq, ss, tC, tS)
        q2 = qr2[i % 2]
        for cpy in range(2):
            k.op("dve", lambda e, q2=q2, cpy=cpy: e.tensor_copy(out=q2[:, :, cpy, :],
                                                                in_=qr[:].rearrange("p (h d) -> p h d", h=8)), [qr], [q2])
        silu_op(k, sz[i % 3], sz[i % 3][:], qb, qb[:, 512:1024], szt, szt[:])

    def pre_pe(i):
        q2, q_t = qr2[i % 2], qT[i % 2]
        for hf in range(2):
            for h4 in range(4):
                h = hf * 4 + h4
                k.op("pe", lambda e, h=h, h4=h4, q2=q2: e.transpose(out=psQ[:, h4 * 128:(h4 + 1) * 128],
                                                                    in_=q2[:, h, :, :].rearrange("p a d -> p (a d)"),
                                                                    identity=ident[:]), [q2, ident], [psQ])
            copy_op(k, "dve", q_t[:, hf * 4:(hf + 1) * 4, :], psQ[:].rearrange("p (h t) -> p h t", h=4), [psQ], [q_t])

    def post_a(i, gi):
        po, o_t = psO[gi], oT[gi]
        copy_op(k, "dve", o_t[:], po[:], [po], [o_t])

    def post_b(i, gi):
        o_t = oT[gi]
        for j in range(4):
            k.op("pe", lambda e, j=j, o_t=o_t: e.transpose(out=psK[:, j * 128:(j + 1) * 128], in_=o_t[:, j * 128:(j + 1) * 128],
                                                           identity=ident[:]), [o_t, ident], [psK])
        o3 = psK[:].rearrange("p (h d) -> p h d", h=4)
        k.op("dve", lambda e, o3=o3, gi=gi: e.reciprocal(out=rec[:, gi * 4:(gi + 1) * 4], in_=o3[:, :, 64]), [psK], [rec])
        k.op("dve", lambda e, o3=o3, gi=gi: e.tensor_tensor(
            out=ob[:, gi * 256:(gi + 1) * 256].rearrange("p (h d) -> p h d", h=4), in0=o3[:, :, 0:64],
            in1=rec[:, gi * 4:(gi + 1) * 4].unsqueeze(2).to_broadcast([128, 4, 64]), op=ALU.mult), [psK, rec], [ob])

    def post_c(i):
        k.op("dve", lambda e, i=i: e.tensor_tensor(out=ob[:], in0=ob[:], in1=sz[i % 3][:], op=ALU.mult), [ob, sz[i % 3]], [ob])
        for cc in range(4):
            k.op("pe", lambda e, cc=cc: e.transpose(out=psQ[:, cc * 128:(cc + 1) * 128], in_=ob[:, cc * 128:(cc + 1) * 128],
                                                    identity=ident[:]), [ob, ident], [psQ])
        copy_op(k, "dve", yb[:], psQ[:].rearrange("p (c t) -> p c t", c=4), [psQ], [yb])
        k.dma("pool", g.MT[0:4, :, i * 128:(i + 1) * 128].rearrange("c p t -> p c t"), yb[:], yb, reads=[yb])

    pre_dve(0)
    pre_pe(0)
    n = 0
    deferred = []
    SKP = 1
    for i in range(NT):
        q_t = qT[i % 2]
        for gi in range(2):
            po = psO[gi]
            if gi == 0 and i + 1 < NT:
                pre_dve(i + 1)
            pend = []
            for m in range(NP + SKP):
                if m < NP:
                    pair = []
                    for p_ in range(2):
                        cix = 2 * m + p_
                        p = psS[n % 4]
                        pb = pt[n % 6]
                        n += 1
                        k.op("pe", lambda e, p=p, gi=gi, m=m, p_=p_, q_t=q_t: e.matmul(
                            p[:], lhsT=kT2[p_ * 64:(p_ + 1) * 64, gi, m * 128:(m + 1) * 128],
                            rhs=q_t[p_ * 64:(p_ + 1) * 64, gi * 4:(gi + 1) * 4, :], start=True, stop=True),
                            [kTb[cix], q_t], [p])
                        pair.append((p, pb, cix))
                    for (p, pb, cix) in pair:
                        k.op("act", lambda e, p=p, pb=pb: e.activation(out=pb[:], in_=p[:], func=AF.Exp, scale=0.125), [p], [pb])
                    pend.append(pair)
                if m >= SKP:
                    for (p, pb_, c_) in pend.pop(0):
                        k.op("pe", lambda e, po=po, pb_=pb_, c_=c_, gi=gi: e.matmul(
                            po[:], lhsT=va[:, c_, gi, :], rhs=pb_[:], start=(c_ == 0), stop=(c_ == NT - 1)),
                            [pb_, vab[c_]], [po])
                if m == min(1, NP - 1):
                    for fn in deferred:
                        fn()
                    deferred = []
            post_a(i, gi)
            deferred.append(lambda i=i, gi=gi: post_b(i, gi))
            if gi == 0 and i + 1 < NT:
                deferred.append(lambda i=i: pre_pe(i + 1))
            if gi == 1:
                deferred.append(lambda i=i: post_c(i))
    for fn in deferred:
        fn()
    k.end()


def phase_scan(k, g, l, kind):
    L, NT = g.L, g.NT
    hg = (kind == "hgrn")
    U = 4 if hg else 1
    SC = 128 // U
    k.begin()
    c = load_cst(k, g, ["ident", "ones", "sel_f", "sel_b", "rowm"] + (["trib_f", "trib_b"] if hg else ["tri_f", "tri_b"]))
    ident, ones, rowm = c["ident"], c["ones"], c["rowm"]
    identb = k.sb("identb", [128, 128], BF16)
    copy_op(k, "dve", identb[:], ident[:], [ident], [identb])
    onesb = k.sb("onesb", [128, 128], BF16)
    copy_op(k, "dve", onesb[:], ones[:], [ones], [onesb])
    selb = [k.sb("selb%d" % d, [128, 4], BF16) for d in range(2)]
    copy_op(k, "dve", selb[0][:], c["sel_f"][:], [c["sel_f"]], [selb[0]])
    copy_op(k, "dve", selb[1][:], c["sel_b"][:], [c["sel_b"]], [selb[1]])
    gs = 1.0
    lb = oml = None
    if hg:
        pr = load_prm(k, g, l, ["lbl", "hnorm"])
        gain = pr["hnorm"]
        if l > 0:
            lb = k.sb("lb", [128, 512]); oml = k.sb("oml", [128, 512])
            k.op("dve", lambda e: e.tensor_tensor(out=lb[:], in0=pr["lbl"][:, 0:512], in1=pr["lbl"][:, 512:1024],
                                                  op=ALU.subtract), [pr["lbl"]], [lb])
            k.op("act", lambda e: e.activation(out=lb[:], in_=lb[:], func=AF.Exp), [lb], [lb])
            k.op("dve", lambda e: e.tensor_scalar(out=lb[:], in0=lb[:], scalar1=1.0, scalar2=0.0, op0=ALU.add, op1=ALU.add),
                 [lb], [lb])
            k.op("dve", lambda e: e.reciprocal(out=lb[:], in_=lb[:]), [lb], [lb])
            k.op("dve", lambda e: e.tensor_scalar(out=oml[:], in0=lb[:], scalar1=-1.0, scalar2=1.0, op0=ALU.mult,
                                                  op1=ALU.add), [lb], [oml])
            gs = -1.0
        scl = 128.0 ** -0.5
        OFS, mt0, zcol = (g.OFH, g.OFH2), 4, H_Z
    else:
        pr = load_prm(k, g, l, ["w2", "gkb", "gnorm"])
        gain = pr["gnorm"]
        scl = 64.0 ** -0.5
        OFS, mt0, zcol = (g.OFG, g.OFG2), 12, G_Z
    ofd = [[Buf("ofd%d_%d" % (d, i)) for i in range(NT)] for d in range(2)]

    def stream(d):
        sfx = "_%d" % d
        def sb(n, shape, dt=F32):
            return k.sb(n + sfx, shape, dt)
        in0 = [sb("in0%d" % i, [128, 512]) for i in range(2)]
        in1 = [sb("in1%d" % i, [128, 512]) for i in range(2)]
        in2 = [sb("in2%d" % i, [128, 512]) for i in range(2)]
        qs = sb("qs", [128, 512]); fg = sb("fg", [128, 512]); gl = sb("gl", [128, 512]); kk = sb("kk", [128, 512])
        t1 = sb("t1", [128, 512])
        vb = sb("vb", [128, 512], BF16)
        eb = sb("eb", [128, 512], BF16); enb = sb("enb", [128, 512], BF16); sqb = sb("sqb", [128, 512], BF16)
        qt = sb("qt", [128, 512], BF16); kt = sb("kt", [128, 512], BF16)
        ktm = [sb("ktm%d" % u, [128, 512], BF16) for u in range(U)] if U > 1 else [kt]
        qT = sb("qT", [128, 512], BF16); kT = sb("kT", [128, 512], BF16)
        Eb = sb("Eb", [128, 16])
        msT = sb("msT", [128, 512], BF16)
        S = sb("S", [128, 512]); tmp = sb("tmp", [128, 512])
        Sbf = [sb("Sbf%d" % i, [128, 512], BF16) for i in range(2)]
        of = sb("of", [128, 512]); o = sb("o", [128, 512]); rs = sb("rs", [128, 512])
        zt = sb("zt", [128, 512]); sz = sb("sz", [128, 512]); yb = sb("yb", [128, 512], BF16)
        if not hg:
            low = [sb("low%d" % i, [128, 128]) for i in range(2)]
            lowT = sb("lowT", [128, 128]); gx = sb("gx", [128, 256])
        ps_a = k.ps("ps_a" + sfx, [128, 512]); ps_t = k.ps("ps_t" + sfx, [128, 1024], BF16)
        ps_c = k.ps("ps_c" + sfx, [128, 512]); ps_o = k.ps("ps_o" + sfx, [128, 512])
        if not hg:
            for b in (qs, kk, gl, low[0], low[1]):
                k.op("pool", lambda e, b=b: e.memset(b[:], 0.0), [], [b])
        if hg:
            trib = c["trib_f"] if d == 0 else c["trib_b"]
        else:
            trib = c["tri_f"] if d == 0 else c["tri_b"]
        sel = selb[d]
        k.op("dve", lambda e: e.memset(S[:], 0.0), [], [S])
        k.op("dve", lambda e: e.memset(Sbf[0][:], 0.0), [], [Sbf[0]])
        nsb = 0
        order = list(range(NT)) if d == 0 else list(range(NT - 1, -1, -1))
        for it, i in enumerate(order):
            a0, a1, a2 = in0[it % 2], in1[it % 2], in2[it % 2]
            rows = slice(i * 128, (i + 1) * 128)
            if hg:
                k.dma("sp", a0[:], g.H[rows, H_Q:H_Q + 512], a0, writes=[a0])
                fc = H_FF if d == 0 else H_FB
                k.dma("sp", a1[:], g.H[rows, fc:fc + 512], a1, writes=[a1])
                k.dma("sp", a2[:], g.H[rows, H_I:H_I + 512], a2, writes=[a2])
                silu_op(k, qs, qs[:], a0, a0[:], t1, t1[:])
                yield
                k.op("act", lambda e, a1=a1: e.activation(out=fg[:], in_=a1[:], func=AF.Exp, scale=-1.0), [a1], [fg])
                k.op("act", lambda e: e.activation(out=gl[:], in_=fg[:], func=AF.Ln, bias=1.0, scale=1.0), [fg], [gl])
                if l == 0:
                    k.op("dve", lambda e, a1=a1: e.tensor_tensor(out=kk[:], in0=a1[:], in1=gl[:], op=ALU.add), [a1, gl], [kk])
                    k.op("act", lambda e: e.activation(out=kk[:], in_=kk[:], func=AF.Exp, scale=-1.0), [kk], [kk])
                else:
                    k.op("act", lambda e: e.activation(out=fg[:], in_=gl[:], func=AF.Exp, scale=-1.0), [gl], [fg])
                    k.op("dve", lambda e: e.tensor_tensor(out=fg[:], in0=fg[:], in1=oml[:], op=ALU.mult), [fg, oml], [fg])
                    k.op("dve", lambda e: e.tensor_tensor(out=fg[:], in0=fg[:], in1=lb[:], op=ALU.add), [fg, lb], [fg])
                    k.op("act", lambda e: e.activation(out=gl[:], in_=fg[:], func=AF.Ln), [fg], [gl])
                    k.op("dve", lambda e: e.tensor_scalar(out=kk[:], in0=fg[:], scalar1=-1.0, scalar2=1.0, op0=ALU.mult,
                                                          op1=ALU.add), [fg], [kk])
            else:
                lw = low[it % 2]
                k.dma("sp", a0[:], g.H[rows, G_Q:G_Q + 512], a0, writes=[a0])
                k.dma("sp", a2[:], g.H[rows, G_V:G_V + 512], a2, writes=[a2])
                lc = G_LF if d == 0 else G_LB
                k.dma("sp", lw[:, 0:16], g.H[rows, lc:lc + 16], lw, writes=[lw])
                k.op("dve", lambda e, a0=a0: e.tensor_copy(out=qs[:].rearrange("p (h x) -> p h x", h=4)[:, :, 0:64],
                                                           in_=a0[:, 0:256].rearrange("p (h x) -> p h x", h=4)), [a0], [qs])
                k.op("dve", lambda e, a0=a0: e.tensor_copy(out=kk[:].rearrange("p (h x) -> p h x", h=4)[:, :, 0:64],
                                                           in_=a0[:, 256:512].rearrange("p (h x) -> p h x", h=4)), [a0], [kk])
                k.op("pe", lambda e, lw=lw: e.transpose(out=ps_a[:, 16:144], in_=lw[:], identity=ident[:]), [lw, ident], [ps_a])
                copy_op(k, "act", lowT[:], ps_a[:, 16:144], [ps_a], [lowT])
                yield
                k.op("pe", lambda e: e.matmul(ps_a[:, 144:400], lhsT=lowT[:], rhs=pr["w2"][:, d * 256:(d + 1) * 256],
                                              start=True, stop=True), [lowT, pr["w2"]], [ps_a])
                k.op("dve", lambda e: e.tensor_tensor(out=gx[:], in0=ps_a[:, 144:400], in1=pr["gkb"][:, d * 256:(d + 1) * 256],
                                                      op=ALU.add), [ps_a, pr["gkb"]], [gx])
                k.op("act", lambda e: e.activation(out=gx[:], in_=gx[:], func=AF.Exp, scale=-1.0), [gx], [gx])
                k.op("act", lambda e: e.activation(out=gx[:], in_=gx[:], func=AF.Ln, bias=1.0, scale=1.0), [gx], [gx])
                k.op("dve", lambda e: e.tensor_scalar(out=gl[:].rearrange("p (h x) -> p h x", h=4)[:, :, 0:64],
                                                      in0=gx[:].rearrange("p (h x) -> p h x", h=4), scalar1=1.0 / 16,
                                                      scalar2=0.0, op0=ALU.mult, op1=ALU.add), [gx], [gl])
            copy_op(k, "act", vb[:], a2[:], [a2], [vb])
            yield
            k.op("pe", lambda e: e.matmul(ps_a[:], lhsT=trib[:], rhs=gl[:], start=True, stop=True), [trib, gl], [ps_a])
            k.op("act", lambda e: e.activation(out=eb[:], in_=ps_a[:], func=AF.Exp, scale=-gs), [ps_a], [eb])
            k.op("act", lambda e: e.activation(out=enb[:], in_=ps_a[:], func=AF.Exp, scale=gs), [ps_a], [enb])
            yield
            k.op("dve", lambda e: e.scalar_tensor_tensor(out=qt[:], in0=qs[:], scalar=scl, in1=eb[:], op0=ALU.mult,
                                                         op1=ALU.mult), [qs, eb], [qt])
            if U > 1:
                k.op("dve", lambda e: e.tensor_tensor(out=tmp[:], in0=kk[:], in1=enb[:], op=ALU.mult), [kk, enb], [tmp])
                copy_op(k, "act", kt[:], tmp[:], [tmp], [kt])
                for u in range(U):
                    k.op("pool", lambda e, u=u: e.tensor_scalar(out=ktm[u][:], in0=tmp[:], scalar1=rowm[:, u:u + 1], scalar2=0.0,
                                                                op0=ALU.mult, op1=ALU.add), [tmp, rowm], [ktm[u]])
            else:
                k.op("dve", lambda e: e.tensor_tensor(out=kt[:], in0=kk[:], in1=enb[:], op=ALU.mult), [kk, enb], [kt])
            yield
            for h in range(4):
                hs = slice(h * 128, (h + 1) * 128)
                k.op("pe", lambda e, hs=hs: e.transpose(out=ps_t[:, hs], in_=qt[:, hs], identity=identb[:]), [qt, identb], [ps_t])
            copy_op(k, "act", qT[:], ps_t[:, 0:512], [ps_t], [qT])
            for h in range(4):
                hs = slice(h * 128, (h + 1) * 128)
                hs2 = slice(512 + h * 128, 512 + (h + 1) * 128)
                k.op("pe", lambda e, hs=hs, hs2=hs2: e.transpose(out=ps_t[:, hs2], in_=kt[:, hs], identity=identb[:]),
                     [kt, identb], [ps_t])
            copy_op(k, "dve", kT[:], ps_t[:, 512:1024], [ps_t], [kT])
            for h in range(4):
                hs = slice(h * 128, (h + 1) * 128)
                k.op("pe", lambda e, hs=hs, h=h: e.matmul(ps_a[:, h * 4:(h + 1) * 4], lhsT=eb[:, hs], rhs=sel[:],
                                                          start=True, stop=True), [eb, sel], [ps_a])
            copy_op(k, "dve", Eb[:], ps_a[:, 0:16], [ps_a], [Eb])
            yield
            for h in range(4):
                hs = slice(h * 128, (h + 1) * 128)
                k.op("pe", lambda e, hs=hs: e.matmul(ps_c[:, hs], lhsT=kT[:, hs], rhs=qT[:, hs], start=True, stop=True),
                     [kT, qT], [ps_c])
            k.op("dve", lambda e: e.tensor_tensor(out=msT[:].rearrange("p (h t) -> p h t", h=4),
                                                  in0=ps_c[:].rearrange("p (h t) -> p h t", h=4),
                                                  in1=trib[:].unsqueeze(1).to_broadcast([128, 4, 128]), op=ALU.mult),
                 [ps_c, trib], [msT])
            yield
            for h in range(4):
                hs = slice(h * 128, (h + 1) * 128)
                k.op("pe", lambda e, hs=hs, h=h: e.matmul(ps_o[:, hs], lhsT=vb[:, hs], rhs=msT[:, hs], start=(h == 0), stop=False,
                                                          skip_group_check=True), [vb, msT], [ps_o])
            us = list(range(U)) if d == 0 else list(range(U - 1, -1, -1))
            for ui, u in enumerate(us):
                sbf = Sbf[nsb % 2]
                ecol = u if U > 1 else (3 if d == 0 else 0)
                for h in range(4):
                    hs = slice(h * 128, (h + 1) * 128)
                    cs = slice(h * 128 + SC * u, h * 128 + SC * u + SC)
                    k.op("pe", lambda e, hs=hs, cs=cs, sbf=sbf, ui=ui: e.matmul(
                        ps_o[:, cs], lhsT=sbf[:, hs], rhs=qT[:, cs], start=False, stop=(ui == U - 1), skip_group_check=True),
                        [sbf, qT], [ps_o])
                for h in range(4):
                    hs = slice(h * 128, (h + 1) * 128)
                    k.op("pe", lambda e, hs=hs, u=u: e.matmul(ps_c[:, hs], lhsT=ktm[u][:, hs], rhs=vb[:, hs], start=True,
                                                              stop=True), [ktm[u], vb], [ps_c])
                k.op("dve", lambda e: e.tensor_tensor(out=tmp[:], in0=S[:], in1=ps_c[:], op=ALU.add), [S, ps_c], [tmp])
                k.op("dve", lambda e, ecol=ecol: e.tensor_tensor(
                    out=S[:].rearrange("p (h v) -> p h v", h=4), in0=tmp[:].rearrange("p (h v) -> p h v", h=4),
                    in1=Eb[:].rearrange("p (h u) -> p h u", h=4)[:, :, ecol:ecol + 1].to_broadcast([128, 4, 128]), op=ALU.mult),
                    [tmp, Eb], [S])
                nsb += 1
                copy_op(k, "act", Sbf[nsb % 2][:], S[:], [S], [Sbf[nsb % 2]])
                yield
            first = (it < NT // 2) if NT > 1 else (d == 0)
            if NT % 2 == 1 and it == NT // 2:
                first = (d == 0)
            if first:
                ofv = OFS[d][:, :, i * 128:(i + 1) * 128].rearrange("h v t -> v h t")
                copy_op(k, "act", of[:], ps_o[:], [ps_o], [of])
                k.dma("pool", ofv, of[:].rearrange("p (h t) -> p h t", h=4), of, reads=[of], writes=[ofd[d][i]])
                yield
            else:
                ofv = OFS[1 - d][:, :, i * 128:(i + 1) * 128].rearrange("h v t -> v h t")
                k.dma("sp", of[:].rearrange("p (h t) -> p h t", h=4), ofv, of, reads=[ofd[1 - d][i]], writes=[of])
                k.dma("sp", zt[:], g.H[rows, zcol:zcol + 512], zt, writes=[zt])
                k.op("dve", lambda e: e.tensor_tensor(out=o[:], in0=of[:], in1=ps_o[:], op=ALU.add), [of, ps_o], [o])
                k.op("pool", lambda e: e.tensor_tensor(out=sqb[:], in0=o[:], in1=o[:], op=ALU.mult), [o], [sqb])
                yield
                if hg:
                    for h in range(4):
                        hs = slice(h * 128, (h + 1) * 128)
                        k.op("pe", lambda e, hs=hs, h=h: e.matmul(ps_a[:, 0:128], lhsT=onesb[:], rhs=sqb[:, hs], start=(h == 0),
                                                                  stop=(h == 3)), [onesb, sqb], [ps_a])
                    rstd_op(k, rs, rs[:, 0:128], ps_a, ps_a[:, 0:128], 1.0 / 512, 1e-6)
                    rsv = rs[:, 0:128].unsqueeze(1).to_broadcast([128, 4, 128])
                else:
                    for h in range(4):
                        hs = slice(h * 128, (h + 1) * 128)
                        k.op("pe", lambda e, hs=hs: e.matmul(ps_a[:, hs], lhsT=onesb[:], rhs=sqb[:, hs], start=True, stop=True),
                             [onesb, sqb], [ps_a])
                    rstd_op(k, rs, rs[:], ps_a, ps_a[:], 1.0 / 128, 1e-6)
                    rsv = rs[:].rearrange("p (h t) -> p h t", h=4)
                yield
                silu_op(k, sz, sz[:], zt, zt[:], t1, t1[:])
                for h in range(4):
                    hs = slice(h * 128, (h + 1) * 128)
                    k.op("pe", lambda e, hs=hs: e.transpose(out=ps_c[:, hs], in_=sz[:, hs], identity=ident[:]), [sz, ident], [ps_c])
                yield
                o3 = o[:].rearrange("p (h t) -> p h t", h=4)
                k.op("dve", lambda e, rsv=rsv, o3=o3: e.tensor_tensor(out=o3, in0=o3, in1=rsv, op=ALU.mult), [o, rs], [o])
                k.op("dve", lambda e: e.tensor_tensor(out=o[:], in0=o[:], in1=ps_c[:], op=ALU.mult), [o, ps_c], [o])
                gv = gain[:, 0:4].unsqueeze(2).to_broadcast([128, 4, 128])
                k.op("dve", lambda e, gv=gv, o3=o3: e.tensor_tensor(out=yb[:].rearrange("p (h t) -> p h t", h=4), in0=o3,
                                                                    in1=gv, op=ALU.mult), [o, gain], [yb])
                k.dma("pool", g.MT[mt0:mt0 + 4, :, i * 128:(i + 1) * 128].rearrange("c p t -> p c t"),
                      yb[:].rearrange("p (h t) -> p h t", h=4), yb, reads=[yb])
                yield

    gens = [stream(0), stream(1)]
    alive = [True, True]
    while any(alive):
        for j in range(2):
            if alive[j]:
                try:
                    next(gens[j])
                except StopIteration:
                    alive[j] = False
    k.end()


def phase_ssd(k, g, l):
    L, NT = g.L, g.NT
    k.begin()
    c = load_cst(k, g, ["ident", "ones", "tri_f", "tri_b", "neg_f", "neg_b"])
    ident, ones = c["ident"], c["ones"]
    pr = load_prm(k, g, l, ["convw", "convb", "dtb", "alog", "dsk", "snorm"])
    identb = k.sb("identb", [128, 128], BF16)
    copy_op(k, "dve", identb[:], ident[:], [ident], [identb])
    xsT = k.sb("xsT", [128, 4, L])
    bcT = k.sb("bcT", [128, 4, L], BF16)
    xsTb = k.view(xsT, "xsTall"); bcTb = k.view(bcT, "bcTall")
    CW = min(L, 1024)
    xc = [k.sb("xc%d" % i, [128, CW + 4]) for i in range(2)]
    acc = [k.sb("acc%d" % i, [128, CW]) for i in range(2)]
    n = 0
    for cc in range(8):
        for t0 in range(0, L, CW):
            xb, ab = xc[n % 2], acc[n % 2]
            n += 1
            lo = max(t0 - 2, 0); hi = min(t0 + CW + 2, L)
            if lo > t0 - 2:
                k.op("pool", lambda e, xb=xb: e.memset(xb[:, 0:2], 0.0), [], [xb])
            if hi < t0 + CW + 2:
                k.op("pool", lambda e, xb=xb: e.memset(xb[:, CW + 2:CW + 4], 0.0), [], [xb])
            k.dma("sp", xb[:, lo - (t0 - 2):hi - (t0 - 2)], g.XBCT[cc * 128:(cc + 1) * 128, lo:hi], xb, writes=[xb])
            for j in range(5):
                wj = pr["convw"][:, cc * 5 + j:cc * 5 + j + 1]
                if j == 0:
                    k.op("dve", lambda e, xb=xb, ab=ab, wj=wj: e.tensor_scalar(out=ab[:], in0=xb[:, 0:CW], scalar1=wj, scalar2=0.0,
                                                                               op0=ALU.mult, op1=ALU.add), [xb, pr["convw"]], [ab])
                else:
                    k.op("dve", lambda e, xb=xb, ab=ab, wj=wj, j=j: e.scalar_tensor_tensor(
                        out=ab[:], in0=xb[:, j:j + CW], scalar=wj, in1=ab[:], op0=ALU.mult, op1=ALU.add), [xb, ab, pr["convw"]], [ab])
            dst = xsT[:, cc, t0:t0 + CW] if cc < 4 else bcT[:, cc - 4, t0:t0 + CW]
            db = xsTb if cc < 4 else bcTb
            k.op("act", lambda e, ab=ab, dst=dst, cc=cc: e.activation(out=dst, in_=ab[:], func=AF.Silu, bias=pr["convb"][:, cc:cc + 1],
                                                                      scale=1.0), [ab, pr["convb"]], [db])
    A = k.sb("A", [128, 16])
    k.op("act", lambda e: e.activation(out=A[:], in_=pr["alog"][:], func=AF.Exp), [pr["alog"]], [A])
    k.op("dve", lambda e: e.tensor_scalar(out=A[:], in0=A[:], scalar1=-1.0, scalar2=0.0, op0=ALU.mult, op1=ALU.add), [A], [A])
    dtr = [k.sb("dtr%d" % i, [128, 8]) for i in range(2)]
    dt = k.sb("dt", [128, 8]); a = k.sb("a", [128, 8]); nacs = k.sb("nacs", [128, 8]); ea = k.sb("ea", [128, 8])
    dte = k.sb("dte", [128, 8]); eend = k.sb("eend", [128, 8])
    xst = k.sb("xst", [128, 512]); xdt = k.sb("xdt", [128, 512], BF16); xw = k.sb("xw", [128, 512], BF16)
    btok = k.sb("btok", [128, 256], BF16)
    Dm = k.sb("Dm", [128, 1024], BF16); M = k.sb("M", [128, 1024], BF16)
    y = k.sb("y", [128, 512]); yf = k.sb("yf", [128, 512]); hst = k.sb("hst", [128, 512]); hbf = k.sb("hbf", [128, 512], BF16)
    zt = k.sb("zt", [128, 512]); sz = k.sb("sz", [128, 512]); sq = k.sb("sq", [128, 512]); ssq = k.sb("ssq", [128, 2])
    yb = k.sb("yb", [128, 512], BF16)
    ps_m = k.ps("ps_m", [128, 512]); ps_x = k.ps("ps_x", [128, 512]); ps_bt = k.ps("ps_bt", [128, 1024], BF16)
    ps_L = [k.ps("ps_L%d" % i, [128, 512]) for i in range(2)]
    ps_y = k.ps("ps_y", [128, 512]); ps_f = k.ps("ps_f", [128, 512]); ps_h = k.ps("ps_h", [128, 512])
    yfd = [Buf("yfd%d" % i) for i in range(NT)]
    for d in range(2):
        tri = c["tri_f"] if d == 0 else c["tri_b"]
        neg = c["neg_f"] if d == 0 else c["neg_b"]
        k.op("dve", lambda e: e.memset(hst[:], 0.0), [], [hst])
        k.op("dve", lambda e: e.memset(hbf[:], 0.0), [], [hbf])
        order = list(range(NT)) if d == 0 else list(range(NT - 1, -1, -1))
        for it, i in enumerate(order):
            rows = slice(i * 128, (i + 1) * 128)
            ts = slice(i * 128, (i + 1) * 128)
            db = dtr[it % 2]
            dc = S_DTF if d == 0 else S_DTB
            k.dma("sp", db[:], g.H[rows, dc:dc + 8], db, writes=[db])
            k.op("dve", lambda e, db=db, d=d: e.tensor_tensor(out=dt[:], in0=db[:], in1=pr["dtb"][:, d * 8:(d + 1) * 8], op=ALU.add),
                 [db, pr["dtb"]], [dt])
            k.op("act", lambda e: e.activation(out=dt[:], in_=dt[:], func=AF.Exp), [dt], [dt])
            k.op("act", lambda e: e.activation(out=dt[:], in_=dt[:], func=AF.Ln, bias=1.0, scale=1.0), [dt], [dt])
            k.op("dve", lambda e, d=d: e.tensor_tensor(out=a[:], in0=dt[:], in1=A[:, d * 8:(d + 1) * 8], op=ALU.mult), [dt, A], [a])
            k.op("pe", lambda e, tri=tri, neg=neg: e.matmul(ps_m[:, 256:264], lhsT=tri[:], rhs=a[:], start=True, stop=True), [tri, a], [ps_m])
            k.op("pe", lambda e: e.matmul(ps_m[:, 264:272], lhsT=ones[:], rhs=a[:], start=True, stop=True), [ones, a], [ps_m])
            k.op("dve", lambda e: e.tensor_scalar(out=nacs[:], in0=ps_m[:, 256:264], scalar1=-1.0, scalar2=0.0, op0=ALU.mult, op1=ALU.add),
                 [ps_m], [nacs])
            k.op("act", lambda e: e.activation(out=ea[:], in_=ps_m[:, 256:264], func=AF.Exp), [ps_m], [ea])
            k.op("dve", lambda e: e.tensor_tensor(out=dte[:], in0=ps_m[:, 264:272], in1=nacs[:], op=ALU.add), [ps_m, nacs], [dte])
            k.op("act", lambda e: e.activation(out=dte[:], in_=dte[:], func=AF.Exp), [dte], [dte])
            k.op("act", lambda e: e.activation(out=eend[:], in_=ps_m[:, 264:272], func=AF.Exp), [ps_m], [eend])
            for cc in range(4):
                k.op("pe", lambda e, cc=cc, ts=ts: e.transpose(out=ps_x[:, cc * 128:(cc + 1) * 128], in_=xsT[:, cc, ts],
                                                               identity=ident[:]), [xsTb, ident], [ps_x])
            copy_op(k, "act", xst[:], ps_x[:], [ps_x], [xst])
            x3 = xst[:].rearrange("p (h d) -> p h d", h=8)
            k.op("dve", lambda e, x3=x3: e.tensor_tensor(out=xdt[:].rearrange("p (h d) -> p h d", h=8), in0=x3,
                                                         in1=dt[:].unsqueeze(2).to_broadcast([128, 8, 64]), op=ALU.mult),
                 [xst, dt], [xdt])
            k.op("dve", lambda e: e.tensor_tensor(out=xw[:].rearrange("p (h d) -> p h d", h=8),
                                                  in0=xdt[:].rearrange("p (h d) -> p h d", h=8),
                                                  in1=dte[:].unsqueeze(2).to_broadcast([128, 8, 64]), op=ALU.mult),
                 [xdt, dte], [xw])
            for gi in range(2):
                k.op("pe", lambda e, gi=gi, ts=ts: e.transpose(out=ps_bt[:, gi * 128:(gi + 1) * 128], in_=bcT[:, gi, ts],
                                                               identity=identb[:]), [bcTb, identb], [ps_bt])
            copy_op(k, "dve", btok[:], ps_bt[:, 0:256], [ps_bt], [btok])
            for gi in range(2):
                k.op("pe", lambda e, gi=gi, ts=ts: e.matmul(ps_m[:, gi * 128:(gi + 1) * 128], lhsT=bcT[:, gi, ts], rhs=bcT[:, 2 + gi, ts],
                                                            start=True, stop=True), [bcTb], [ps_m])
            for q in range(2):
                pl = ps_L[q]
                k.op("pe", lambda e, tri=tri, neg=neg, pl=pl: e.matmul(pl[:], lhsT=ident[:], rhs=neg[:], start=True, stop=False, skip_group_check=True), [ident, neg], [pl])
                for h4 in range(4):
                    h = q * 4 + h4
                    k.op("pe", lambda e, pl=pl, h4=h4, h=h, tri=tri: e.matmul(pl[:, h4 * 128:(h4 + 1) * 128],
                                                                     lhsT=a[:, h:h + 1].to_broadcast([128, 128]), rhs=tri[:],
                                                                     start=False, stop=True, skip_group_check=True), [a, tri], [pl])
                for h4 in range(4):
                    h = q * 4 + h4
                    k.op("act", lambda e, pl=pl, h4=h4, h=h: e.activation(out=Dm[:, h * 128:(h + 1) * 128],
                                                                          in_=pl[:, h4 * 128:(h4 + 1) * 128], func=AF.Exp,
                                                                          bias=nacs[:, h:h + 1], scale=1.0), [pl, nacs], [Dm])
                k.op("dve", lambda e, q=q: e.tensor_tensor(
                    out=M[:, q * 512:(q + 1) * 512].rearrange("p (h t) -> p h t", h=4),
                    in0=Dm[:, q * 512:(q + 1) * 512].rearrange("p (h t) -> p h t", h=4),
                    in1=ps_m[:, q * 128:(q + 1) * 128].unsqueeze(1).to_broadcast([128, 4, 128]), op=ALU.mult), [Dm, ps_m], [M])
            for h in range(8):
                k.op("pe", lambda e, h=h: e.matmul(ps_y[:, h * 64:(h + 1) * 64], lhsT=M[:, h * 128:(h + 1) * 128],
                                                   rhs=xdt[:, h * 64:(h + 1) * 64], start=True, stop=True), [M, xdt], [ps_y])
            for gi in range(2):
                k.op("pe", lambda e, gi=gi, ts=ts: e.matmul(ps_f[:, gi * 256:(gi + 1) * 256], lhsT=bcT[:, 2 + gi, ts],
                                                            rhs=hbf[:, gi * 256:(gi + 1) * 256], start=True, stop=True),
                     [bcTb, hbf], [ps_f])
            k.op("dve", lambda e: e.tensor_tensor(out=y[:].rearrange("p (h d) -> p h d", h=8),
                                                  in0=ps_f[:].rearrange("p (h d) -> p h d", h=8),
                                                  in1=ea[:].unsqueeze(2).to_broadcast([128, 8, 64]), op=ALU.mult), [ps_f, ea], [y])
            k.op("dve", lambda e: e.tensor_tensor(out=y[:], in0=y[:], in1=ps_y[:], op=ALU.add), [y, ps_y], [y])
            for gi in range(2):
                k.op("pe", lambda e, gi=gi: e.matmul(ps_h[:, gi * 256:(gi + 1) * 256], lhsT=btok[:, gi * 128:(gi + 1) * 128],
                                                     rhs=xw[:, gi * 256:(gi + 1) * 256], start=True, stop=True), [btok, xw], [ps_h])
            k.op("dve", lambda e: e.tensor_tensor(out=hst[:].rearrange("p (h d) -> p h d", h=8),
                                                  in0=hst[:].rearrange("p (h d) -> p h d", h=8),
                                                  in1=eend[:].unsqueeze(2).to_broadcast([128, 8, 64]), op=ALU.mult), [hst, eend], [hst])
            k.op("dve", lambda e: e.tensor_tensor(out=hst[:], in0=hst[:], in1=ps_h[:], op=ALU.add), [hst, ps_h], [hst])
            copy_op(k, "act", hbf[:], hst[:], [hst], [hbf])
            if d == 0:
                k.dma("pool", g.YF[rows, :], y[:], y, reads=[y], writes=[yfd[i]])
            else:
                k.dma("sp", yf[:], g.YF[rows, :], yf, reads=[yfd[i]], writes=[yf])
                k.dma("sp", zt[:], g.H[rows, S_Z:S_Z + 512], zt, writes=[zt])
                k.op("dve", lambda e: e.tensor_tensor(out=y[:], in0=y[:], in1=yf[:], op=ALU.add), [y, yf], [y])
                k.op("dve", lambda e, x3=x3: e.tensor_tensor(out=sq[:].rearrange("p (h d) -> p h d", h=8), in0=x3,
                                                             in1=pr["dsk"][:].unsqueeze(2).to_broadcast([128, 8, 64]), op=ALU.mult),
                     [xst, pr["dsk"]], [sq])
                k.op("dve", lambda e: e.tensor_tensor(out=y[:], in0=y[:], in1=sq[:], op=ALU.add), [y, sq], [y])
                silu_op(k, sz, sz[:], zt, zt[:], sq, sq[:])
                k.op("dve", lambda e: e.tensor_tensor(out=y[:], in0=y[:], in1=sz[:], op=ALU.mult), [y, sz], [y])
                k.op("dve", lambda e: e.tensor_tensor(out=sq[:], in0=y[:], in1=y[:], op=ALU.mult), [y], [sq])
                k.op("dve", lambda e: e.tensor_reduce(out=ssq[:, 0:1], in_=sq[:], axis=AX.X, op=ALU.add), [sq], [ssq])
                rstd_op(k, ssq, ssq[:, 0:1], ssq, ssq[:, 0:1], 1.0 / 512, 1e-6)
                k.op("dve", lambda e: e.scalar_tensor_tensor(out=y[:], in0=y[:], scalar=ssq[:, 0:1], in1=pr["snorm"][:],
                                                             op0=ALU.mult, op1=ALU.mult), [y, ssq, pr["snorm"]], [y])
                for cc in range(4):
                    k.op("pe", lambda e, cc=cc: e.transpose(out=ps_x[:, cc * 128:(cc + 1) * 128], in_=y[:, cc * 128:(cc + 1) * 128],
                                                            identity=ident[:]), [y, ident], [ps_x])
                copy_op(k, "act", yb[:], ps_x[:], [ps_x], [yb])
                k.dma("pool", g.MT[8:12, :, i * 128:(i + 1) * 128].rearrange("c p t -> p c t"),
                      yb[:].rearrange("p (c t) -> p c t", c=4), yb, reads=[yb])
    k.end()


def phase_out(k, g, l, x_ap, y_ap):
    L, NT = g.L, g.NT
    k.begin()
    pr = load_prm(k, g, l, ["lng", "lnb"])
    wo = k.sb("wo", [128, 16, 1024], BF16)
    wf = [k.sb("wf%d" % i, [128, 4, 1024]) for i in range(2)]
    wov = g.w_out[l].rearrange("(c p) n -> p c n", p=128)
    for q in range(4):
        w = wf[q % 2]
        k.dma("sp", w[:], wov[:, q * 4:(q + 1) * 4, :], w, writes=[w])
        copy_op(k, _alt(q), wo[:, q * 4:(q + 1) * 4, :], w[:], [w], [wo])
    mt = [k.sb("mt%d" % i, [128, 16, 128], BF16) for i in range(2)]
    xt = [k.sb("xt%d" % i, [128, 1024]) for i in range(2)]
    r = k.sb("r", [128, 1024]); sq = k.sb("sq", [128, 1024]); st = k.sb("st", [128, 4])
    ot = [k.sb("ot%d" % i, [128, 1024]) for i in range(2)]
    ps = [k.ps("ps%d" % i, [128, 512]) for i in range(4)]
    for i in range(NT):
        m, x, o = mt[i % 2], xt[i % 2], ot[i % 2]
        rows = slice(i * 128, (i + 1) * 128)
        k.dma("sp", m[:], g.MT[:, :, i * 128:(i + 1) * 128].rearrange("c p t -> p c t"), m, writes=[m])
        k.dma("sp", x[:], x_ap[rows, :], x, writes=[x])
        for hf in range(2):
            p = ps[(i % 2) * 2 + hf]
            for cc in range(16):
                k.op("pe", lambda e, p=p, cc=cc, hf=hf, m=m: e.matmul(p[:], lhsT=m[:, cc, :], rhs=wo[:, cc, hf * 512:(hf + 1) * 512],
                                                                      start=(cc == 0), stop=(cc == 15)), [m, wo], [p])
            k.op("dve", lambda e, p=p, hf=hf, x=x: e.scalar_tensor_tensor(
                out=r[:, hf * 512:(hf + 1) * 512], in0=x[:, hf * 512:(hf + 1) * 512], scalar=ALPHA, in1=p[:],
                op0=ALU.mult, op1=ALU.add), [x, p], [r])
        k.op("dve", lambda e: e.tensor_reduce(out=st[:, 0:1], in_=r[:], axis=AX.X, op=ALU.add), [r], [st])
        k.op("dve", lambda e: e.tensor_scalar(out=st[:, 1:2], in0=st[:, 0:1], scalar1=-1.0 / 1024, scalar2=0.0, op0=ALU.mult, op1=ALU.add),
             [st], [st])
        k.op("act", lambda e: e.activation(out=r[:], in_=r[:], func=AF.Identity, bias=st[:, 1:2], scale=1.0), [r, st], [r])
        k.op("pool", lambda e: e.tensor_tensor(out=sq[:], in0=r[:], in1=r[:], op=ALU.mult), [r], [sq])
        k.op("dve", lambda e: e.tensor_reduce(out=st[:, 2:3], in_=sq[:], axis=AX.X, op=ALU.add), [sq], [st])
        rstd_op(k, st, st[:, 2:3], st, st[:, 2:3], 1.0 / 1024, 1e-5)
        k.op("dve", lambda e, o=o: e.scalar_tensor_tensor(out=o[:], in0=r[:], scalar=st[:, 2:3], in1=pr["lng"][:], op0=ALU.mult,
                                                          op1=ALU.mult), [r, st, pr["lng"]], [o])
        k.op("pool", lambda e, o=o: e.tensor_tensor(out=o[:], in0=o[:], in1=pr["lnb"][:], op=ALU.add), [o, pr["lnb"]], [o])
        k.dma("pool", y_ap[rows, :], o[:], o, reads=[o])
    k.end()


def build_nc(L, layers=(0, 1), phases="PAHSGO", debug=False):
    nc = bass.Bass("TRN2", target_bir_lowering=False)
    g = G()
    g.L, g.NT = L, L // 128
    kin = "ExternalInput"
    g.x = nc.dram_tensor("x", [L, D], F32, kind=kin).ap()
    g.w_in = nc.dram_tensor("w_in", [DEPTH, D, NIN], F32, kind=kin).ap()
    g.w_out = nc.dram_tensor("w_out", [DEPTH, 2 * D, D], F32, kind=kin).ap()
    g.cst = nc.dram_tensor("cst", [128, NCST], F32, kind=kin).ap()
    g.rope = nc.dram_tensor("rope", [L, 128], F32, kind=kin).ap()
    g.prm = nc.dram_tensor("prm", [DEPTH, 128, NPRM], F32, kind=kin).ap()
    g.y = nc.dram_tensor("y", [L, D], F32, kind="ExternalOutput").ap()
    dk = "ExternalOutput" if debug else "Internal"
    g.H = nc.dram_tensor("H", [L, NIN], F32, kind=dk).ap()
    g.XBCT = nc.dram_tensor("XBCT", [1024, L], F32, kind=dk).ap()
    g.MT = nc.dram_tensor("MT", [16, 128, L], BF16, kind=dk).ap()
    g.OFH = nc.dram_tensor("OFH", [4, 128, L], F32).ap()
    g.OFG = nc.dram_tensor("OFG", [4, 128, L], F32).ap()
    g.OFH2 = nc.dram_tensor("OFH2", [4, 128, L], F32).ap()
    g.OFG2 = nc.dram_tensor("OFG2", [4, 128, L], F32).ap()
    g.YF = nc.dram_tensor("YF", [L, 512], F32).ap()
    g.X1 = nc.dram_tensor("X1", [L, D], F32, kind=dk).ap()
    k = K(nc)
    for l in layers:
        xin = g.x if l == layers[0] else g.X1
        yout = g.X1 if l != layers[-1] else g.y
        if "P" in phases:
            phase_proj(k, g, l, xin)
        if "A" in phases:
            phase_attn(k, g, l)
        if "H" in phases:
            phase_scan(k, g, l, "hgrn")
        if "S" in phases:
            phase_ssd(k, g, l)
        if "G" in phases:
            phase_scan(k, g, l, "gla")
        if "O" in phases:
            phase_out(k, g, l, xin, yout)
    k.emit()
    g.n_instr = k.n_instr
    return nc, g


_NC_CACHE = {}


def kernel(**inp):
    x = np.asarray(inp["x"], np.float32)
    B, L, _ = x.shape
    if L not in _NC_CACHE:
        _NC_CACHE[L] = build_nc(L)[0]
    nc = _NC_CACHE[L]
    cst, rope = make_consts(L)
    prm = make_params({k_: np.asarray(v, np.float32) for k_, v in inp.items()})
    w_in = np.ascontiguousarray(inp["w_in"], np.float32)
    w_out = np.ascontiguousarray(inp["w_out"], np.float32)
    n = 8
    in_maps = []
    for c in range(n):
        in_maps.append({"x": np.ascontiguousarray(x[c % B]), "w_in": w_in, "w_out": w_out, "cst": cst, "rope": rope,
                        "prm": prm})
    res = run_bass_kernel_spmd(nc, in_maps, core_ids=list(range(n)))
    return np.stack([np.asarray(res.results[b]["y"], np.float32) for b in range(B)], axis=0)
```

```python
import contextlib
import numpy as np
import concourse.bass as bass
import concourse.mybir as mybir
from concourse.bass_utils import run_bass_kernel_spmd

F32 = mybir.dt.float32
BF16 = mybir.dt.bfloat16
AF = mybir.ActivationFunctionType
ALU = mybir.AluOpType
AX = mybir.AxisListType

ENGS = ("pe", "act", "dve", "pool", "sp")
SAME_ENGINE_SYNC = True

D = 1024
NIN = 6960
DEPTH = 2
ALPHA = float((2 * DEPTH) ** 0.25)
A_Q, A_K, A_V, A_Z = 0, 512, 640, 768
H_Q, H_FF, H_FB, H_I, H_Z = 1280, 1792, 2304, 2816, 3328
S_XBC, S_DTF, S_DTB, S_Z = 3840, 4864, 4872, 4880
G_Q, G_K, G_V, G_LF, G_LB, G_Z = 5392, 5648, 5904, 6416, 6432, 6448
NEG = -1.0e5


class Buf:
    def __init__(self, name, t=None):
        self.name = name
        self.t = t
        self.w = []
        self.r = []
        self.dsem = {}

    def __getitem__(self, k):
        return self.t[k]


class K:
    def __init__(self, nc):
        self.nc = nc
        self.es = contextlib.ExitStack()
        self.ph = None
        self.prog = {e: [] for e in ENGS}
        self.cnt = {}
        self.sem = {}
        self.waited = {e: {} for e in ENGS}
        self.free_dsems = {}
        self.ph_bufs = []
        self.uid = 0
        for e in ("pe", "act", "dve", "pool"):
            self._mksem("c_" + e)
        self.n_instr = 0

    def _mksem(self, name):
        s = self.es.enter_context(self.nc.semaphore(name))
        self.sem[name] = s
        self.cnt[name] = 0
        return name

    def begin(self):
        self.ph = contextlib.ExitStack()
        self.ph_bufs = []

    def end(self):
        self.barrier()
        self.emit_block()
        for b in self.ph_bufs:
            for q_, s_ in b.dsem.items():
                self.free_dsems.setdefault(q_, []).append(s_)
            b.dsem = {}
        self.ph.close()
        self.ph = None

    def barrier(self):
        for e in ENGS:
            waits = []
            for s, c in self.cnt.items():
                if c > 0 and self.waited[e].get(s, 0) < c:
                    if s == "c_" + e:
                        continue
                    waits.append((s, c))
                    self.waited[e][s] = c
            if waits:
                self.prog[e].append((waits, None, None))

    def sb(self, name, shape, dtype=F32):
        self.uid += 1
        t = self.ph.enter_context(self.nc.sbuf_tensor("%s_%d" % (name, self.uid), list(shape), dtype))
        b = Buf(name, t)
        self.ph_bufs.append(b)
        return b

    def ps(self, name, shape, dtype=F32):
        self.uid += 1
        t = self.ph.enter_context(self.nc.psum_tensor("%s_%d" % (name, self.uid), list(shape), dtype))
        b = Buf(name, t)
        self.ph_bufs.append(b)
        return b

    def view(self, base, name):
        b = Buf(name, base.t)
        self.ph_bufs.append(b)
        return b

    def _deps(self, eng, reads, writes):
        evs = []
        for b in reads:
            evs.extend(b.w)
        for b in writes:
            evs.extend(b.w)
            evs.extend(b.r)
        need = {}
        for (s, v, e) in evs:
            if e == eng and (eng == "pe" or (not SAME_ENGINE_SYNC and eng in ("act", "dve", "pool"))):
                continue
            if need.get(s, 0) < v:
                need[s] = v
        out = []
        wd = self.waited[eng]
        for s, v in need.items():
            if wd.get(s, 0) >= v:
                continue
            wd[s] = v
            out.append((s, v))
        return out

    def _record(self, ev, reads, writes):
        for b in writes:
            b.w = [ev]
            b.r = []
        for b in reads:
            if b not in writes:
                b.r.append(ev)
                if len(b.r) > 48:
                    m = {}
                    for (s, v, e) in b.r:
                        if s not in m or m[s][1] < v:
                            m[s] = (s, v, e)
                    b.r = list(m.values())

    def op(self, eng, fn, reads=(), writes=()):
        waits = self._deps(eng, reads, writes)
        s = "c_" + eng
        self.cnt[s] += 1
        ev = (s, self.cnt[s], eng)
        self.prog[eng].append((waits, fn, (s, 1)))
        self._record(ev, reads, writes)
        self.n_instr += 1
        return ev

    def dma(self, q, out_ap, in_ap, sbuf, reads=(), writes=()):
        if q not in sbuf.dsem:
            fl = self.free_dsems.get(q, [])
            if fl:
                sbuf.dsem[q] = fl.pop()
            else:
                sbuf.dsem[q] = self._mksem("d%s_%d" % (q, len(self.sem)))
        s = sbuf.dsem[q]
        waits = self._deps(q, reads, writes)
        self.cnt[s] += 16
        ev = (s, self.cnt[s], "dma")
        self.prog[q].append((waits, lambda e: e.dma_start(out=out_ap, in_=in_ap), (s, 16)))
        self._record(ev, reads, writes)
        self.n_instr += 1
        return ev

    def coll(self, kind, groups, in_ap, out_ap, reads=(), writes=()):
        if "coll" not in self.sem:
            self._mksem("coll")
        s = "coll"
        waits = self._deps("pool", reads, writes)
        self.cnt[s] += 16
        ev = (s, self.cnt[s], "dma")
        self.prog["pool"].append((waits, lambda e: e.collective_compute(
            kind, ALU.bypass, replica_groups=groups, ins=[in_ap], outs=[out_ap]), (s, 16)))
        self._record(ev, reads, writes)
        self.n_instr += 1
        return ev

    def emit_block(self):
        nc = self.nc
        sem = self.sem
        prog = self.prog

        def run(engname):
            def body(eng):
                for (waits, fn, inc) in prog[engname]:
                    for (s, v) in waits:
                        eng.wait_ge(sem[s], v)
                    if fn is not None:
                        ins = fn(eng)
                        ins.then_inc(sem[inc[0]], inc[1])
            return body

        with nc.Block() as block:
            block.tensor(run("pe"))
            block.scalar(run("act"))
            block.vector(run("dve"))
            block.gpsimd(run("pool"))
            block.sync(run("sp"))
        self.prog = {e: [] for e in ENGS}

    def emit(self):
        self.es.close()


CST = {}
_o = 0
for _n, _w in (("ident", 128), ("ones", 128), ("trib_f", 128), ("trib_b", 128), ("sel_f", 4), ("sel_b", 4),
               ("rowm", 4), ("tri_f", 128), ("tri_b", 128), ("neg_f", 512), ("neg_b", 512)):
    CST[_n] = (_o, _w)
    _o += _w
NCST = _o

PRM = {}
_o = 0
for _n, _w in (("aqn", 64), ("akn", 64), ("lbl", 1024), ("hnorm", 4), ("convw", 40), ("convb", 8), ("dtb", 16),
               ("alog", 16), ("dsk", 8), ("snorm", 512), ("w2", 512), ("gkb", 512), ("gnorm", 4), ("lng", 1024),
               ("lnb", 1024)):
    PRM[_n] = (_o, _w)
    _o += _w
NPRM = _o


def make_consts(L):
    c = np.zeros((128, NCST), np.float32)
    s = np.arange(128)[:, None]
    t = np.arange(128)[None, :]
    same = (s // 32) == (t // 32)

    def put(n, a):
        o, w = CST[n]
        c[:, o:o + w] = a
    put("ident", np.eye(128))
    put("ones", np.ones((128, 128)))
    put("trib_f", (same & (s <= t)).astype(np.float32))
    put("trib_b", (same & (s >= t)).astype(np.float32))
    sf = np.zeros((128, 4)); sb_ = np.zeros((128, 4)); rm = np.zeros((128, 4))
    for u in range(4):
        sf[32 * u + 31, u] = 1
        sb_[32 * u, u] = 1
        rm[32 * u:32 * u + 32, u] = 1
    put("sel_f", sf); put("sel_b", sb_); put("rowm", rm)
    put("tri_f", (s <= t).astype(np.float32))
    put("tri_b", (s >= t).astype(np.float32))
    put("neg_f", np.tile(np.where(s <= t, 0.0, NEG), (1, 4)))
    put("neg_b", np.tile(np.where(s >= t, 0.0, NEG), (1, 4)))
    pos = np.arange(L)
    row = (pos // 64).astype(np.float32)
    col = (pos % 64).astype(np.float32)
    inv = np.power(np.float32(10000.0), -np.arange(0, 32, 2, dtype=np.float32) / np.float32(32)).astype(np.float32)
    ar = row[:, None] * inv
    ac = col[:, None] * inv
    cos = np.concatenate([np.cos(ar), np.cos(ar), np.cos(ac), np.cos(ac)], 1).astype(np.float32)
    sin = np.concatenate([np.sin(ar), np.sin(ar), np.sin(ac), np.sin(ac)], 1).astype(np.float32)
    return c, np.ascontiguousarray(np.concatenate([cos, sin], 1))


def make_params(inp):
    p = np.zeros((DEPTH, 128, NPRM), np.float32)

    def put(l, n, a):
        o, w = PRM[n]
        p[l, :, o:o + w] = a
    for l in range(DEPTH):
        put(l, "aqn", np.broadcast_to(inp["attn_q_norm"][l][None, :], (128, 64)))
        put(l, "akn", np.broadcast_to(inp["attn_k_norm"][l][None, :], (128, 64)))
        put(l, "lbl", np.broadcast_to(inp["hgrn_lb_logits"].reshape(1, 1024), (128, 1024)))
        put(l, "hnorm", inp["hgrn_norm"][l].reshape(4, 128).T)
        put(l, "convw", inp["ssd_conv_w"][l].reshape(5, 8, 128).transpose(2, 1, 0).reshape(128, 40))
        put(l, "convb", inp["ssd_conv_b"][l].reshape(8, 128).T)
        put(l, "dtb", np.broadcast_to(inp["ssd_dt_bias"][l].reshape(1, 16), (128, 16)))
        put(l, "alog", np.broadcast_to(inp["ssd_a_log"][l].reshape(1, 16), (128, 16)))
        put(l, "dsk", np.broadcast_to(inp["ssd_d"][l].reshape(1, 8), (128, 8)))
        put(l, "snorm", np.broadcast_to(inp["ssd_norm"][l].reshape(1, 512), (128, 512)))
        w2 = np.zeros((128, 512), np.float32)
        w2[:16] = inp["gla_gk_w2"][l].transpose(1, 0, 2).reshape(16, 512)
        put(l, "w2", w2)
        put(l, "gkb", np.broadcast_to(inp["gla_gk_b"][l].reshape(1, 512), (128, 512)))
        put(l, "gnorm", np.repeat(inp["gla_norm"][l].reshape(128, 1), 4, axis=1))
        put(l, "lng", np.broadcast_to(inp["ln_g"][l][None, :], (128, 1024)))
        put(l, "lnb", np.broadcast_to(inp["ln_b"][l][None, :], (128, 1024)))
    return p


class G:
    pass


def _alt(i):
    return "act" if i % 2 == 0 else "dve"


def copy_op(k, eng, out_ap, in_ap, reads, writes):
    if eng == "act":
        k.op("act", lambda e: e.copy(out=out_ap, in_=in_ap), reads, writes)
    else:
        k.op(eng, lambda e: e.tensor_copy(out=out_ap, in_=in_ap), reads, writes)


def load_cst(k, g, names):
    out = {}
    for n in names:
        o, w = CST[n]
        b = k.sb("c_" + n, [128, w])
        k.dma("sp", b[:], g.cst[:, o:o + w], b, writes=[b])
        out[n] = b
    return out


def load_prm(k, g, l, names):
    out = {}
    for n in names:
        o, w = PRM[n]
        b = k.sb("p_" + n, [128, w])
        k.dma("sp", b[:], g.prm[l, :, o:o + w], b, writes=[b])
        out[n] = b
    return out


def rstd_op(k, out_b, out_ap, in_b, in_ap, scale, eps):
    k.op("dve", lambda e: e.tensor_scalar(out=out_ap, in0=in_ap, scalar1=scale, scalar2=eps, op0=ALU.mult, op1=ALU.add),
         [in_b], [out_b])
    k.op("act", lambda e: e.activation(out=out_ap, in_=out_ap, func=AF.Ln), [out_b], [out_b])
    k.op("act", lambda e: e.activation(out=out_ap, in_=out_ap, func=AF.Exp, scale=-0.5), [out_b], [out_b])


def silu_op(k, out_b, out_ap, in_b, in_ap, tmp_b, tmp_ap):
    k.op("act", lambda e: e.activation(out=tmp_ap, in_=in_ap, func=AF.Exp, scale=-1.0), [in_b], [tmp_b])
    k.op("act", lambda e: e.activation(out=tmp_ap, in_=tmp_ap, func=AF.Ln, bias=1.0, scale=1.0), [tmp_b], [tmp_b])
    k.op("act", lambda e: e.activation(out=tmp_ap, in_=tmp_ap, func=AF.Exp, scale=-1.0), [tmp_b], [tmp_b])
    k.op("dve", lambda e: e.tensor_tensor(out=out_ap, in0=in_ap, in1=tmp_ap, op=ALU.mult), [in_b, tmp_b], [out_b])


def phase_proj(k, g, l, x_ap):
    L, NT = g.L, g.NT
    k.begin()
    c = load_cst(k, g, ["ident"])
    ident = c["ident"]
    xT = k.sb("xT", [128, 8, L], BF16)
    xTb = [k.view(xT, "xT%d" % i) for i in range(NT)]
    xs = [k.sb("xs%d" % i, [128, 1024]) for i in range(2)]
    psT = [k.ps("psT%d" % i, [128, 512]) for i in range(2)]
    for i in range(NT):
        xb = xs[i % 2]
        k.dma("sp", xb[:], x_ap[i * 128:(i + 1) * 128, :], xb, writes=[xb])
        for hf in range(2):
            p = psT[hf]
            for cc in range(4):
                k.op("pe", lambda e, p=p, cc=cc, hf=hf, xb=xb: e.transpose(
                    out=p[:, cc * 128:(cc + 1) * 128], in_=xb[:, (hf * 4 + cc) * 128:(hf * 4 + cc + 1) * 128],
                    identity=ident[:]), [xb, ident], [p])
            copy_op(k, _alt(hf), xT[:, hf * 4:(hf + 1) * 4, i * 128:(i + 1) * 128],
                    p[:].rearrange("p (c t) -> p c t", c=4), [p], [xTb[i]])
    blocks = [(0, 512, 0), (512, 512, 0), (1024, 256, 0)]
    blocks += [(1280 + 512 * j, 512, 0) for j in range(5)]
    blocks += [(3840, 512, 1), (4352, 512, 1), (4864, 16, 0), (4880, 512, 0)]
    blocks += [(5392, 512, 0), (5904, 512, 0), (6416, 32, 0), (6448, 512, 0)]
    Wf = [k.sb("Wf%d" % i, [128, 8, 512]) for i in range(2)]
    Wb = [k.sb("Wb%d" % i, [128, 8, 512], BF16) for i in range(2)]
    stg = [k.sb("stg%d" % i, [128, 512]) for i in range(3)]
    psM = [k.ps("psM%d" % i, [128, 512]) for i in range(4)]
    wv = g.w_in[l].rearrange("(c p) n -> p c n", p=128)

    def loadw(bi):
        c0, wd, _ = blocks[bi]
        wf = Wf[bi % 2]
        k.dma("sp", wf[:, :, 0:wd], wv[:, :, c0:c0 + wd], wf, writes=[wf])

    def castw(bi):
        c0, wd, _ = blocks[bi]
        wf, wb = Wf[bi % 2], Wb[bi % 2]
        k.op("dve", lambda e: e.tensor_copy(out=wb[:, :, 0:wd], in_=wf[:, :, 0:wd]), [wf], [wb])

    loadw(0)
    castw(0)
    n = 0
    for bi, (c0, wd, mode) in enumerate(blocks):
        wb = Wb[bi % 2]
        if bi + 1 < len(blocks):
            loadw(bi + 1)
        if mode == 0:
            units = [(i,) for i in range(NT)]
        else:
            units = [(cc, tb) for cc in range(wd // 128) for tb in range(L // 512)]
        for ui, u in enumerate(units):
            if ui == len(units) // 2 and bi + 1 < len(blocks):
                castw(bi + 1)
            p = psM[n % 4]
            s = stg[n % 3]
            if mode == 0:
                i = u[0]
                for cc in range(8):
                    k.op("pe", lambda e, p=p, cc=cc, i=i, wb=wb, wd=wd: e.matmul(
                        p[:, 0:wd], lhsT=xT[:, cc, i * 128:(i + 1) * 128], rhs=wb[:, cc, 0:wd],
                        start=(cc == 0), stop=(cc == 7)), [xTb[i], wb], [p])
                copy_op(k, _alt(n), s[:, 0:wd], p[:, 0:wd], [p], [s])
                k.dma("act" if _alt(n) == "act" else "pool", g.H[i * 128:(i + 1) * 128, c0:c0 + wd], s[:, 0:wd], s, reads=[s])
            else:
                cc2, tb = u
                for cc in range(8):
                    k.op("pe", lambda e, p=p, cc=cc, cc2=cc2, tb=tb, wb=wb: e.matmul(
                        p[:, 0:512], lhsT=wb[:, cc, cc2 * 128:(cc2 + 1) * 128], rhs=xT[:, cc, tb * 512:(tb + 1) * 512],
                        start=(cc == 0), stop=(cc == 7)), [xTb[tb * 4 + j] for j in range(4)] + [wb], [p])
                copy_op(k, _alt(n), s[:, 0:512], p[:, 0:512], [p], [s])
                r0 = c0 - S_XBC + cc2 * 128
                k.dma("pool", g.XBCT[r0:r0 + 128, tb * 512:(tb + 1) * 512], s[:, 0:512], s, reads=[s])
            n += 1
    k.end()


def norm_rope(k, nh, src, dst, gain, rope, sq, ss, tC, tS):
    W = nh * 64
    s3 = src[:, 0:W].rearrange("p (h d) -> p h d", h=nh)
    k.op("dve", lambda e: e.tensor_tensor(out=sq[:, 0:W], in0=src[:, 0:W], in1=src[:, 0:W], op=ALU.mult), [src], [sq])
    k.op("dve", lambda e: e.tensor_reduce(out=ss[:, 0:nh], in_=sq[:, 0:W].rearrange("p (h d) -> p h d", h=nh),
                                          axis=AX.X, op=ALU.add), [sq], [ss])
    rstd_op(k, ss, ss[:, 0:nh], ss, ss[:, 0:nh], 1.0 / 64, 1e-6)
    q3 = sq[:, 0:W].rearrange("p (h d) -> p h d", h=nh)
    k.op("dve", lambda e: e.tensor_tensor(out=q3, in0=s3, in1=ss[:, 0:nh].unsqueeze(2).to_broadcast([128, nh, 64]),
                                          op=ALU.mult), [src, ss], [sq])
    k.op("dve", lambda e: e.tensor_tensor(out=q3, in0=q3, in1=gain[:, 0:64].unsqueeze(1).to_broadcast([128, nh, 64]),
                                          op=ALU.mult), [sq, gain], [sq])
    k.op("dve", lambda e: e.tensor_tensor(out=tC[:, 0:W].rearrange("p (h d) -> p h d", h=nh), in0=q3,
                                          in1=rope[:, 0:64].unsqueeze(1).to_broadcast([128, nh, 64]), op=ALU.mult),
         [sq, rope], [tC])
    k.op("dve", lambda e: e.tensor_tensor(out=tS[:, 0:W].rearrange("p (h d) -> p h d", h=nh), in0=q3,
                                          in1=rope[:, 64:128].unsqueeze(1).to_broadcast([128, nh, 64]), op=ALU.mult),
         [sq, rope], [tS])
    o4 = dst[:, 0:W].rearrange("p (a b d) -> p a b d", b=2, d=16)
    c4 = tC[:, 0:W].rearrange("p (a b d) -> p a b d", b=2, d=16)
    s4 = tS[:, 0:W].rearrange("p (a b d) -> p a b d", b=2, d=16)
    k.op("dve", lambda e: e.tensor_tensor(out=o4[:, :, 0, :], in0=c4[:, :, 0, :], in1=s4[:, :, 1, :], op=ALU.subtract),
         [tC, tS], [dst])
    k.op("dve", lambda e: e.tensor_tensor(out=o4[:, :, 1, :], in0=c4[:, :, 1, :], in1=s4[:, :, 0, :], op=ALU.add),
         [tC, tS], [dst])


def phase_attn(k, g, l):
    L, NT = g.L, g.NT
    NP = NT // 2
    k.begin()
    c = load_cst(k, g, ["ident"])
    ident = c["ident"]
    pr = load_prm(k, g, l, ["aqn", "akn"])
    kT2 = k.sb("kT2", [128, 2, NP * 128], BF16)
    kTb = [k.view(kT2, "kT%d" % i) for i in range(NT)]
    va = k.sb("va", [128, NT, 2, 128], BF16)
    vab = [k.view(va, "va%d" % i) for i in range(NT)]
    k.op("pool", lambda e: e.memset(va[:], 1.0), [], vab)
    kv = [k.sb("kv%d" % i, [128, 256]) for i in range(2)]
    rp = [k.sb("rp%d" % i, [128, 128]) for i in range(2)]
    sq = k.sb("sq", [128, 512]); ss = k.sb("ss", [128, 8]); tC = k.sb("tC", [128, 512]); tS = k.sb("tS", [128, 512])
    kr = k.sb("kr", [128, 128]); krp = k.sb("krp", [128, 2, 128])
    k.op("pool", lambda e: e.memset(krp[:], 0.0), [], [krp])
    psK = k.ps("psK", [128, 512])
    for i in range(NT):
        kvb, rpb = kv[i % 2], rp[i % 2]
        p_, j = i % 2, i // 2
        k.dma("sp", kvb[:], g.H[i * 128:(i + 1) * 128, A_K:A_K + 256], kvb, writes=[kvb])
        k.dma("sp", rpb[:], g.rope[i * 128:(i + 1) * 128, :], rpb, writes=[rpb])
        norm_rope(k, 2, kvb, kr, pr["akn"], rpb, sq, ss, tC, tS)
        k.op("dve", lambda e, p_=p_: e.tensor_copy(out=krp[:, :, p_ * 64:(p_ + 1) * 64],
                                                   in_=kr[:, 0:128].rearrange("p (g d) -> p g d", g=2)), [kr], [krp])
        for h in range(2):
            k.op("pe", lambda e, h=h: e.transpose(out=psK[:, h * 128:(h + 1) * 128], in_=krp[:, h, :],
                                                  identity=ident[:]), [krp, ident], [psK])
        copy_op(k, "act", kT2[p_ * 64:(p_ + 1) * 64, :, j * 128:(j + 1) * 128],
                psK[p_ * 64:(p_ + 1) * 64, 0:256].rearrange("p (h t) -> p h t", h=2), [psK], [kTb[i]])
        k.op("dve", lambda e, i=i, kvb=kvb: e.tensor_copy(out=va[:, i, :, 0:64],
                                                          in_=kvb[:, 128:256].rearrange("p (h d) -> p h d", h=2)),
             [kvb], [vab[i]])
    qz = [k.sb("qz%d" % i, [128, 1024]) for i in range(2)]
    qr = k.sb("qr", [128, 512])
    qr2 = [k.sb("qr2_%d" % i, [128, 8, 2, 64]) for i in range(2)]
    qT = [k.sb("qT%d" % i, [128, 8, 128], BF16) for i in range(2)]
    sz = [k.sb("sz%d" % i, [128, 512]) for i in range(3)]
    pt = [k.sb("pt%d" % i, [128, 512], BF16) for i in range(6)]
    oT = [k.sb("oT%d" % i, [128, 512]) for i in range(2)]
    rec = k.sb("rec", [128, 8]); ob = k.sb("ob", [128, 512]); yb = k.sb("yb", [128, 4, 128], BF16)
    szt = k.sb("szt", [128, 512])
    psQ = k.ps("psQ", [128, 512])
    psS = [k.ps("psS%d" % i, [128, 512]) for i in range(4)]
    psO = [k.ps("psO%d" % i, [128, 512]) for i in range(2)]

    def pre_dve(i):
        qb, rpb = qz[i % 2], rp[i % 2]
        k.dma("sp", qb[:, 0:512], g.H[i * 128:(i + 1) * 128, A_Q:A_Q + 512], qb, writes=[qb])
        k.dma("sp", qb[:, 512:1024], g.H[i * 128:(i + 1) * 128, A_Z:A_Z + 512], qb, writes=[qb])
        k.dma("sp", rpb[:], g.rope[i * 128:(i + 1) * 128, :], rpb, writes=[rpb])
        norm_rope(k, 8, qb, qr, pr["aqn"], rpb, sq, ss, tC, tS)
        q2 = qr2[i % 2]
        for cpy in range(2):
            k.op("dve", lambda e, q2=q2, cpy=cpy: e.tensor_copy(out=q2[:, :, cpy, :],
                                                                in_=qr[:].rearrange("p (h d) -> p h d", h=8)), [qr], [q2])
        silu_op(k, sz[i % 3], sz[i % 3][:], qb, qb[:, 512:1024], szt, szt[:])

    def pre_pe(i):
        q2, q_t = qr2[i % 2], qT[i % 2]
        for hf in range(2):
            for h4 in range(4):
                h = hf * 4 + h4
                k.op("pe", lambda e, h=h, h4=h4, q2=q2: e.transpose(out=psQ[:, h4 * 128:(h4 + 1) * 128],
                                                                    in_=q2[:, h, :, :].rearrange("p a d -> p (a d)"),
                                                                    identity=ident[:]), [q2, ident], [psQ])
            copy_op(k, "dve", q_t[:, hf * 4:(hf + 1) * 4, :], psQ[:].rearrange("p (h t) -> p h t", h=4), [psQ], [q_t])

    def post_a(i, gi):
        po, o_t = psO[gi], oT[gi]
        copy_op(k, "dve", o_t[:], po[:], [po], [o_t])

    def post_b(i, gi):
        o_t = oT[gi]
        for j in range(4):
            k.op("pe", lambda e, j=j, o_t=o_t: e.transpose(out=psK[:, j * 128:(j + 1) * 128], in_=o_t[:, j * 128:(j + 1) * 128],
                                                           identity=ident[:]), [o_t, ident], [psK])
        o3 = psK[:].rearrange("p (h d) -> p h d", h=4)
        k.op("dve", lambda e, o3=o3, gi=gi: e.reciprocal(out=rec[:, gi * 4:(gi + 1) * 4], in_=o3[:, :, 64]), [psK], [rec])
        k.op("dve", lambda e, o3=o3, gi=gi: e.tensor_tensor(
            out=ob[:, gi * 256:(gi + 1) * 256].rearrange("p (h d) -> p h d", h=4), in0=o3[:, :, 0:64],
            in1=rec[:, gi * 4:(gi + 1) * 4].unsqueeze(2).to_broadcast([128, 4, 64]), op=ALU.mult), [psK, rec], [ob])

    def post_c(i):
        k.op("dve", lambda e, i=i: e.tensor_tensor(out=ob[:], in0=ob[:], in1=sz[i % 3][:], op=ALU.mult), [ob, sz[i % 3]], [ob])
        for cc in range(4):
            k.op("pe", lambda e, cc=cc: e.transpose(out=psQ[:, cc * 128:(cc + 1) * 128], in_=ob[:, cc * 128:(cc + 1) * 128],
                                                    identity=ident[:]), [ob, ident], [psQ])
        copy_op(k, "dve", yb[:], psQ[:].rearrange("p (c t) -> p c t", c=4), [psQ], [yb])
        k.dma("pool", g.MT[0:4, :, i * 128:(i + 1) * 128].rearrange("c p t -> p c t"), yb[:], yb, reads=[yb])

    pre_dve(0)
    pre_pe(0)
    n = 0
    deferred = []
    SKP = 1
    for i in range(NT):
        q_t = qT[i % 2]
        for gi in range(2):
            po = psO[gi]
            if gi == 0 and i + 1 < NT:
                pre_dve(i + 1)
            pend = []
            for m in range(NP + SKP):
                if m < NP:
                    pair = []
                    for p_ in range(2):
                        cix = 2 * m + p_
                        p = psS[n % 4]
                        pb = pt[n % 6]
                        n += 1
                        k.op("pe", lambda e, p=p, gi=gi, m=m, p_=p_, q_t=q_t: e.matmul(
                            p[:], lhsT=kT2[p_ * 64:(p_ + 1) * 64, gi, m * 128:(m + 1) * 128],
                            rhs=q_t[p_ * 64:(p_ + 1) * 64, gi * 4:(gi + 1) * 4, :], start=True, stop=True),
                            [kTb[cix], q_t], [p])
                        pair.append((p, pb, cix))
                    for (p, pb, cix) in pair:
                        k.op("act", lambda e, p=p, pb=pb: e.activation(out=pb[:], in_=p[:], func=AF.Exp, scale=0.125), [p], [pb])
                    pend.append(pair)
                if m >= SKP:
                    for (p, pb_, c_) in pend.pop(0):
                        k.op("pe", lambda e, po=po, pb_=pb_, c_=c_, gi=gi: e.matmul(
                            po[:], lhsT=va[:, c_, gi, :], rhs=pb_[:], start=(c_ == 0), stop=(c_ == NT - 1)),
                            [pb_, vab[c_]], [po])
                if m == min(1, NP - 1):
                    for fn in deferred:
                        fn()
                    deferred = []
            post_a(i, gi)
            deferred.append(lambda i=i, gi=gi: post_b(i, gi))
            if gi == 0 and i + 1 < NT:
                deferred.append(lambda i=i: pre_pe(i + 1))
            if gi == 1:
                deferred.append(lambda i=i: post_c(i))
    for fn in deferred:
        fn()
    k.end()


def phase_scan(k, g, l, kind):
    L, NT = g.L, g.NT
    hg = (kind == "hgrn")
    U = 4 if hg else 1
    SC = 128 // U
    k.begin()
    c = load_cst(k, g, ["ident", "ones", "sel_f", "sel_b", "rowm"] + (["trib_f", "trib_b"] if hg else ["tri_f", "tri_b"]))
    ident, ones, rowm = c["ident"], c["ones"], c["rowm"]
    identb = k.sb("identb", [128, 128], BF16)
    copy_op(k, "dve", identb[:], ident[:], [ident], [identb])
    onesb = k.sb("onesb", [128, 128], BF16)
    copy_op(k, "dve", onesb[:], ones[:], [ones], [onesb])
    selb = [k.sb("selb%d" % d, [128, 4], BF16) for d in range(2)]
    copy_op(k, "dve", selb[0][:], c["sel_f"][:], [c["sel_f"]], [selb[0]])
    copy_op(k, "dve", selb[1][:], c["sel_b"][:], [c["sel_b"]], [selb[1]])
    gs = 1.0
    lb = oml = None
    if hg:
        pr = load_prm(k, g, l, ["lbl", "hnorm"])
        gain = pr["hnorm"]
        if l > 0:
            lb = k.sb("lb", [128, 512]); oml = k.sb("oml", [128, 512])
            k.op("dve", lambda e: e.tensor_tensor(out=lb[:], in0=pr["lbl"][:, 0:512], in1=pr["lbl"][:, 512:1024],
                                                  op=ALU.subtract), [pr["lbl"]], [lb])
            k.op("act", lambda e: e.activation(out=lb[:], in_=lb[:], func=AF.Exp), [lb], [lb])
            k.op("dve", lambda e: e.tensor_scalar(out=lb[:], in0=lb[:], scalar1=1.0, scalar2=0.0, op0=ALU.add, op1=ALU.add),
                 [lb], [lb])
            k.op("dve", lambda e: e.reciprocal(out=lb[:], in_=lb[:]), [lb], [lb])
            k.op("dve", lambda e: e.tensor_scalar(out=oml[:], in0=lb[:], scalar1=-1.0, scalar2=1.0, op0=ALU.mult,
                                                  op1=ALU.add), [lb], [oml])
            gs = -1.0
        scl = 128.0 ** -0.5
        OFS, mt0, zcol = (g.OFH, g.OFH2), 4, H_Z
    else:
        pr = load_prm(k, g, l, ["w2", "gkb", "gnorm"])
        gain = pr["gnorm"]
        scl = 64.0 ** -0.5
        OFS, mt0, zcol = (g.OFG, g.OFG2), 12, G_Z
    ofd = [[Buf("ofd%d_%d" % (d, i)) for i in range(NT)] for d in range(2)]

    def stream(d):
        sfx = "_%d" % d
        def sb(n, shape, dt=F32):
            return k.sb(n + sfx, shape, dt)
        in0 = [sb("in0%d" % i, [128, 512]) for i in range(2)]
        in1 = [sb("in1%d" % i, [128, 512]) for i in range(2)]
        in2 = [sb("in2%d" % i, [128, 512]) for i in range(2)]
        qs = sb("qs", [128, 512]); fg = sb("fg", [128, 512]); gl = sb("gl", [128, 512]); kk = sb("kk", [128, 512])
        t1 = sb("t1", [128, 512])
        vb = sb("vb", [128, 512], BF16)
        eb = sb("eb", [128, 512], BF16); enb = sb("enb", [128, 512], BF16); sqb = sb("sqb", [128, 512], BF16)
        qt = sb("qt", [128, 512], BF16); kt = sb("kt", [128, 512], BF16)
        ktm = [sb("ktm%d" % u, [128, 512], BF16) for u in range(U)] if U > 1 else [kt]
        qT = sb("qT", [128, 512], BF16); kT = sb("kT", [128, 512], BF16)
        Eb = sb("Eb", [128, 16])
        msT = sb("msT", [128, 512], BF16)
        S = sb("S", [128, 512]); tmp = sb("tmp", [128, 512])
        Sbf = [sb("Sbf%d" % i, [128, 512], BF16) for i in range(2)]
        of = sb("of", [128, 512]); o = sb("o", [128, 512]); rs = sb("rs", [128, 512])
        zt = sb("zt", [128, 512]); sz = sb("sz", [128, 512]); yb = sb("yb", [128, 512], BF16)
        if not hg:
            low = [sb("low%d" % i, [128, 128]) for i in range(2)]
            lowT = sb("lowT", [128, 128]); gx = sb("gx", [128, 256])
        ps_a = k.ps("ps_a" + sfx, [128, 512]); ps_t = k.ps("ps_t" + sfx, [128, 1024], BF16)
        ps_c = k.ps("ps_c" + sfx, [128, 512]); ps_o = k.ps("ps_o" + sfx, [128, 512])
        if not hg:
            for b in (qs, kk, gl, low[0], low[1]):
                k.op("pool", lambda e, b=b: e.memset(b[:], 0.0), [], [b])
        if hg:
            trib = c["trib_f"] if d == 0 else c["trib_b"]
        else:
            trib = c["tri_f"] if d == 0 else c["tri_b"]
        sel = selb[d]
        k.op("dve", lambda e: e.memset(S[:], 0.0), [], [S])
        k.op("dve", lambda e: e.memset(Sbf[0][:], 0.0), [], [Sbf[0]])
        nsb = 0
        order = list(range(NT)) if d == 0 else list(range(NT - 1, -1, -1))
        for it, i in enumerate(order):
            a0, a1, a2 = in0[it % 2], in1[it % 2], in2[it % 2]
            rows = slice(i * 128, (i + 1) * 128)
            if hg:
                k.dma("sp", a0[:], g.H[rows, H_Q:H_Q + 512], a0, writes=[a0])
                fc = H_FF if d == 0 else H_FB
                k.dma("sp", a1[:], g.H[rows, fc:fc + 512], a1, writes=[a1])
                k.dma("sp", a2[:], g.H[rows, H_I:H_I + 512], a2, writes=[a2])
                silu_op(k, qs, qs[:], a0, a0[:], t1, t1[:])
                yield
                k.op("act", lambda e, a1=a1: e.activation(out=fg[:], in_=a1[:], func=AF.Exp, scale=-1.0), [a1], [fg])
                k.op("act", lambda e: e.activation(out=gl[:], in_=fg[:], func=AF.Ln, bias=1.0, scale=1.0), [fg], [gl])
                if l == 0:
                    k.op("dve", lambda e, a1=a1: e.tensor_tensor(out=kk[:], in0=a1[:], in1=gl[:], op=ALU.add), [a1, gl], [kk])
                    k.op("act", lambda e: e.activation(out=kk[:], in_=kk[:], func=AF.Exp, scale=-1.0), [kk], [kk])
                else:
                    k.op("act", lambda e: e.activation(out=fg[:], in_=gl[:], func=AF.Exp, scale=-1.0), [gl], [fg])
                    k.op("dve", lambda e: e.tensor_tensor(out=fg[:], in0=fg[:], in1=oml[:], op=ALU.mult), [fg, oml], [fg])
                    k.op("dve", lambda e: e.tensor_tensor(out=fg[:], in0=fg[:], in1=lb[:], op=ALU.add), [fg, lb], [fg])
                    k.op("act", lambda e: e.activation(out=gl[:], in_=fg[:], func=AF.Ln), [fg], [gl])
                    k.op("dve", lambda e: e.tensor_scalar(out=kk[:], in0=fg[:], scalar1=-1.0, scalar2=1.0, op0=ALU.mult,
                                                          op1=ALU.add), [fg], [kk])
            else:
                lw = low[it % 2]
                k.dma("sp", a0[:], g.H[rows, G_Q:G_Q + 512], a0, writes=[a0])
                k.dma("sp", a2[:], g.H[rows, G_V:G_V + 512], a2, writes=[a2])
                lc = G_LF if d == 0 else G_LB
                k.dma("sp", lw[:, 0:16], g.H[rows, lc:lc + 16], lw, writes=[lw])
                k.op("dve", lambda e, a0=a0: e.tensor_copy(out=qs[:].rearrange("p (h x) -> p h x", h=4)[:, :, 0:64],
                                                           in_=a0[:, 0:256].rearrange("p (h x) -> p h x", h=4)), [a0], [qs])
                k.op("dve", lambda e, a0=a0: e.tensor_copy(out=kk[:].rearrange("p (h x) -> p h x", h=4)[:, :, 0:64],
                                                           in_=a0[:, 256:512].rearrange("p (h x) -> p h x", h=4)), [a0], [kk])
                k.op("pe", lambda e, lw=lw: e.transpose(out=ps_a[:, 16:144], in_=lw[:], identity=ident[:]), [lw, ident], [ps_a])
                copy_op(k, "act", lowT[:], ps_a[:, 16:144], [ps_a], [lowT])
                yield
                k.op("pe", lambda e: e.matmul(ps_a[:, 144:400], lhsT=lowT[:], rhs=pr["w2"][:, d * 256:(d + 1) * 256],
                                              start=True, stop=True), [lowT, pr["w2"]], [ps_a])
                k.op("dve", lambda e: e.tensor_tensor(out=gx[:], in0=ps_a[:, 144:400], in1=pr["gkb"][:, d * 256:(d + 1) * 256],
                                                      op=ALU.add), [ps_a, pr["gkb"]], [gx])
                k.op("act", lambda e: e.activation(out=gx[:], in_=gx[:], func=AF.Exp, scale=-1.0), [gx], [gx])
                k.op("act", lambda e: e.activation(out=gx[:], in_=gx[:], func=AF.Ln, bias=1.0, scale=1.0), [gx], [gx])
                k.op("dve", lambda e: e.tensor_scalar(out=gl[:].rearrange("p (h x) -> p h x", h=4)[:, :, 0:64],
                                                      in0=gx[:].rearrange("p (h x) -> p h x", h=4), scalar1=1.0 / 16,
                                                      scalar2=0.0, op0=ALU.mult, op1=ALU.add), [gx], [gl])
            copy_op(k, "act", vb[:], a2[:], [a2], [vb])
            yield
            k.op("pe", lambda e: e.matmul(ps_a[:], lhsT=trib[:], rhs=gl[:], start=True, stop=True), [trib, gl], [ps_a])
            k.op("act", lambda e: e.activation(out=eb[:], in_=ps_a[:], func=AF.Exp, scale=-gs), [ps_a], [eb])
            k.op("act", lambda e: e.activation(out=enb[:], in_=ps_a[:], func=AF.Exp, scale=gs), [ps_a], [enb])
            yield
            k.op("dve", lambda e: e.scalar_tensor_tensor(out=qt[:], in0=qs[:], scalar=scl, in1=eb[:], op0=ALU.mult,
                                                         op1=ALU.mult), [qs, eb], [qt])
            if U > 1:
                k.op("dve", lambda e: e.tensor_tensor(out=tmp[:], in0=kk[:], in1=enb[:], op=ALU.mult), [kk, enb], [tmp])
                copy_op(k, "act", kt[:], tmp[:], [tmp], [kt])
                for u in range(U):
                    k.op("pool", lambda e, u=u: e.tensor_scalar(out=ktm[u][:], in0=tmp[:], scalar1=rowm[:, u:u + 1], scalar2=0.0,
                                                                op0=ALU.mult, op1=ALU.add), [tmp, rowm], [ktm[u]])
            else:
                k.op("dve", lambda e: e.tensor_tensor(out=kt[:], in0=kk[:], in1=enb[:], op=ALU.mult), [kk, enb], [kt])
            yield
            for h in range(4):
                hs = slice(h * 128, (h + 1) * 128)
                k.op("pe", lambda e, hs=hs: e.transpose(out=ps_t[:, hs], in_=qt[:, hs], identity=identb[:]), [qt, identb], [ps_t])
            copy_op(k, "act", qT[:], ps_t[:, 0:512], [ps_t], [qT])
            for h in range(4):
                hs = slice(h * 128, (h + 1) * 128)
                hs2 = slice(512 + h * 128, 512 + (h + 1) * 128)
                k.op("pe", lambda e, hs=hs, hs2=hs2: e.transpose(out=ps_t[:, hs2], in_=kt[:, hs], identity=identb[:]),
                     [kt, identb], [ps_t])
            copy_op(k, "dve", kT[:], ps_t[:, 512:1024], [ps_t], [kT])
            for h in range(4):
                hs = slice(h * 128, (h + 1) * 128)
                k.op("pe", lambda e, hs=hs, h=h: e.matmul(ps_a[:, h * 4:(h + 1) * 4], lhsT=eb[:, hs], rhs=sel[:],
                                                          start=True, stop=True), [eb, sel], [ps_a])
            copy_op(k, "dve", Eb[:], ps_a[:, 0:16], [ps_a], [Eb])
            yield
            for h in range(4):
                hs = slice(h * 128, (h + 1) * 128)
                k.op("pe", lambda e, hs=hs: e.matmul(ps_c[:, hs], lhsT=kT[:, hs], rhs=qT[:, hs], start=True, stop=True),
                     [kT, qT], [ps_c])
            k.op("dve", lambda e: e.tensor_tensor(out=msT[:].rearrange("p (h t) -> p h t", h=4),
                                                  in0=ps_c[:].rearrange("p (h t) -> p h t", h=4),
                                                  in1=trib[:].unsqueeze(1).to_broadcast([128, 4, 128]), op=ALU.mult),
                 [ps_c, trib], [msT])
            yield
            for h in range(4):
                hs = slice(h * 128, (h + 1) * 128)
                k.op("pe", lambda e, hs=hs, h=h: e.matmul(ps_o[:, hs], lhsT=vb[:, hs], rhs=msT[:, hs], start=(h == 0), stop=False,
                                                          skip_group_check=True), [vb, msT], [ps_o])
            us = list(range(U)) if d == 0 else list(range(U - 1, -1, -1))
            for ui, u in enumerate(us):
                sbf = Sbf[nsb % 2]
                ecol = u if U > 1 else (3 if d == 0 else 0)
                for h in range(4):
                    hs = slice(h * 128, (h + 1) * 128)
                    cs = slice(h * 128 + SC * u, h * 128 + SC * u + SC)
                    k.op("pe", lambda e, hs=hs, cs=cs, sbf=sbf, ui=ui: e.matmul(
                        ps_o[:, cs], lhsT=sbf[:, hs], rhs=qT[:, cs], start=False, stop=(ui == U - 1), skip_group_check=True),
                        [sbf, qT], [ps_o])
                for h in range(4):
                    hs = slice(h * 128, (h + 1) * 128)
                    k.op("pe", lambda e, hs=hs, u=u: e.matmul(ps_c[:, hs], lhsT=ktm[u][:, hs], rhs=vb[:, hs], start=True,
                                                              stop=True), [ktm[u], vb], [ps_c])
                k.op("dve", lambda e: e.tensor_tensor(out=tmp[:], in0=S[:], in1=ps_c[:], op=ALU.add), [S, ps_c], [tmp])
                k.op("dve", lambda e, ecol=ecol: e.tensor_tensor(
                    out=S[:].rearrange("p (h v) -> p h v", h=4), in0=tmp[:].rearrange("p (h v) -> p h v", h=4),
                    in1=Eb[:].rearrange("p (h u) -> p h u", h=4)[:, :, ecol:ecol + 1].to_broadcast([128, 4, 128]), op=ALU.mult),
                    [tmp, Eb], [S])
                nsb += 1
                copy_op(k, "act", Sbf[nsb % 2][:], S[:], [S], [Sbf[nsb % 2]])
                yield
            first = (it < NT // 2) if NT > 1 else (d == 0)
            if NT % 2 == 1 and it == NT // 2:
                first = (d == 0)
            if first:
                ofv = OFS[d][:, :, i * 128:(i + 1) * 128].rearrange("h v t -> v h t")
                copy_op(k, "act", of[:], ps_o[:], [ps_o], [of])
                k.dma("pool", ofv, of[:].rearrange("p (h t) -> p h t", h=4), of, reads=[of], writes=[ofd[d][i]])
                yield
            else:
                ofv = OFS[1 - d][:, :, i * 128:(i + 1) * 128].rearrange("h v t -> v h t")
                k.dma("sp", of[:].rearrange("p (h t) -> p h t", h=4), ofv, of, reads=[ofd[1 - d][i]], writes=[of])
                k.dma("sp", zt[:], g.H[rows, zcol:zcol + 512], zt, writes=[zt])
                k.op("dve", lambda e: e.tensor_tensor(out=o[:], in0=of[:], in1=ps_o[:], op=ALU.add), [of, ps_o], [o])
                k.op("pool", lambda e: e.tensor_tensor(out=sqb[:], in0=o[:], in1=o[:], op=ALU.mult), [o], [sqb])
                yield
                if hg:
                    for h in range(4):
                        hs = slice(h * 128, (h + 1) * 128)
                        k.op("pe", lambda e, hs=hs, h=h: e.matmul(ps_a[:, 0:128], lhsT=onesb[:], rhs=sqb[:, hs], start=(h == 0),
                                                                  stop=(h == 3)), [onesb, sqb], [ps_a])
                    rstd_op(k, rs, rs[:, 0:128], ps_a, ps_a[:, 0:128], 1.0 / 512, 1e-6)
                    rsv = rs[:, 0:128].unsqueeze(1).to_broadcast([128, 4, 128])
                else:
                    for h in range(4):
                        hs = slice(h * 128, (h + 1) * 128)
                        k.op("pe", lambda e, hs=hs: e.matmul(ps_a[:, hs], lhsT=onesb[:], rhs=sqb[:, hs], start=True, stop=True),
                             [onesb, sqb], [ps_a])
                    rstd_op(k, rs, rs[:], ps_a, ps_a[:], 1.0 / 128, 1e-6)
                    rsv = rs[:].rearrange("p (h t) -> p h t", h=4)
                yield
                silu_op(k, sz, sz[:], zt, zt[:], t1, t1[:])
                for h in range(4):
                    hs = slice(h * 128, (h + 1) * 128)
                    k.op("pe", lambda e, hs=hs: e.transpose(out=ps_c[:, hs], in_=sz[:, hs], identity=ident[:]), [sz, ident], [ps_c])
                yield
                o3 = o[:].rearrange("p (h t) -> p h t", h=4)
                k.op("dve", lambda e, rsv=rsv, o3=o3: e.tensor_tensor(out=o3, in0=o3, in1=rsv, op=ALU.mult), [o, rs], [o])
                k.op("dve", lambda e: e.tensor_tensor(out=o[:], in0=o[:], in1=ps_c[:], op=ALU.mult), [o, ps_c], [o])
                gv = gain[:, 0:4].unsqueeze(2).to_broadcast([128, 4, 128])
                k.op("dve", lambda e, gv=gv, o3=o3: e.tensor_tensor(out=yb[:].rearrange("p (h t) -> p h t", h=4), in0=o3,
                                                                    in1=gv, op=ALU.mult), [o, gain], [yb])
                k.dma("pool", g.MT[mt0:mt0 + 4, :, i * 128:(i + 1) * 128].rearrange("c p t -> p c t"),
                      yb[:].rearrange("p (h t) -> p h t", h=4), yb, reads=[yb])
                yield

    gens = [stream(0), stream(1)]
    alive = [True, True]
    while any(alive):
        for j in range(2):
            if alive[j]:
                try:
                    next(gens[j])
                except StopIteration:
                    alive[j] = False
    k.end()


def phase_ssd(k, g, l):
    L, NT = g.L, g.NT
    k.begin()
    c = load_cst(k, g, ["ident", "ones", "tri_f", "tri_b", "neg_f", "neg_b"])
    ident, ones = c["ident"], c["ones"]
    pr = load_prm(k, g, l, ["convw", "convb", "dtb", "alog", "dsk", "snorm"])
    identb = k.sb("identb", [128, 128], BF16)
    copy_op(k, "dve", identb[:], ident[:], [ident], [identb])
    xsT = k.sb("xsT", [128, 4, L])
    bcT = k.sb("bcT", [128, 4, L], BF16)
    xsTb = k.view(xsT, "xsTall"); bcTb = k.view(bcT, "bcTall")
    CW = min(L, 1024)
    xc = [k.sb("xc%d" % i, [128, CW + 4]) for i in range(2)]
    acc = [k.sb("acc%d" % i, [128, CW]) for i in range(2)]
    n = 0
    for cc in range(8):
        for t0 in range(0, L, CW):
            xb, ab = xc[n % 2], acc[n % 2]
            n += 1
            lo = max(t0 - 2, 0); hi = min(t0 + CW + 2, L)
            if lo > t0 - 2:
                k.op("pool", lambda e, xb=xb: e.memset(xb[:, 0:2], 0.0), [], [xb])
            if hi < t0 + CW + 2:
                k.op("pool", lambda e, xb=xb: e.memset(xb[:, CW + 2:CW + 4], 0.0), [], [xb])
            k.dma("sp", xb[:, lo - (t0 - 2):hi - (t0 - 2)], g.XBCT[cc * 128:(cc + 1) * 128, lo:hi], xb, writes=[xb])
            for j in range(5):
                wj = pr["convw"][:, cc * 5 + j:cc * 5 + j + 1]
                if j == 0:
                    k.op("dve", lambda e, xb=xb, ab=ab, wj=wj: e.tensor_scalar(out=ab[:], in0=xb[:, 0:CW], scalar1=wj, scalar2=0.0,
                                                                               op0=ALU.mult, op1=ALU.add), [xb, pr["convw"]], [ab])
                else:
                    k.op("dve", lambda e, xb=xb, ab=ab, wj=wj, j=j: e.scalar_tensor_tensor(
                        out=ab[:], in0=xb[:, j:j + CW], scalar=wj, in1=ab[:], op0=ALU.mult, op1=ALU.add), [xb, ab, pr["convw"]], [ab])
            dst = xsT[:, cc, t0:t0 + CW] if cc < 4 else bcT[:, cc - 4, t0:t0 + CW]
            db = xsTb if cc < 4 else bcTb
            k.op("act", lambda e, ab=ab, dst=dst, cc=cc: e.activation(out=dst, in_=ab[:], func=AF.Silu, bias=pr["convb"][:, cc:cc + 1],
                                                                      scale=1.0), [ab, pr["convb"]], [db])
    A = k.sb("A", [128, 16])
    k.op("act", lambda e: e.activation(out=A[:], in_=pr["alog"][:], func=AF.Exp), [pr["alog"]], [A])
    k.op("dve", lambda e: e.tensor_scalar(out=A[:], in0=A[:], scalar1=-1.0, scalar2=0.0, op0=ALU.mult, op1=ALU.add), [A], [A])
    dtr = [k.sb("dtr%d" % i, [128, 8]) for i in range(2)]
    dt = k.sb("dt", [128, 8]); a = k.sb("a", [128, 8]); nacs = k.sb("nacs", [128, 8]); ea = k.sb("ea", [128, 8])
    dte = k.sb("dte", [128, 8]); eend = k.sb("eend", [128, 8])
    xst = k.sb("xst", [128, 512]); xdt = k.sb("xdt", [128, 512], BF16); xw = k.sb("xw", [128, 512], BF16)
    btok = k.sb("btok", [128, 256], BF16)
    Dm = k.sb("Dm", [128, 1024], BF16); M = k.sb("M", [128, 1024], BF16)
    y = k.sb("y", [128, 512]); yf = k.sb("yf", [128, 512]); hst = k.sb("hst", [128, 512]); hbf = k.sb("hbf", [128, 512], BF16)
    zt = k.sb("zt", [128, 512]); sz = k.sb("sz", [128, 512]); sq = k.sb("sq", [128, 512]); ssq = k.sb("ssq", [128, 2])
    yb = k.sb("yb", [128, 512], BF16)
    ps_m = k.ps("ps_m", [128, 512]); ps_x = k.ps("ps_x", [128, 512]); ps_bt = k.ps("ps_bt", [128, 1024], BF16)
    ps_L = [k.ps("ps_L%d" % i, [128, 512]) for i in range(2)]
    ps_y = k.ps("ps_y", [128, 512]); ps_f = k.ps("ps_f", [128, 512]); ps_h = k.ps("ps_h", [128, 512])
    yfd = [Buf("yfd%d" % i) for i in range(NT)]
    for d in range(2):
        tri = c["tri_f"] if d == 0 else c["tri_b"]
        neg = c["neg_f"] if d == 0 else c["neg_b"]
        k.op("dve", lambda e: e.memset(hst[:], 0.0), [], [hst])
        k.op("dve", lambda e: e.memset(hbf[:], 0.0), [], [hbf])
        order = list(range(NT)) if d == 0 else list(range(NT - 1, -1, -1))
        for it, i in enumerate(order):
            rows = slice(i * 128, (i + 1) * 128)
            ts = slice(i * 128, (i + 1) * 128)
            db = dtr[it % 2]
            dc = S_DTF if d == 0 else S_DTB
            k.dma("sp", db[:], g.H[rows, dc:dc + 8], db, writes=[db])
            k.op("dve", lambda e, db=db, d=d: e.tensor_tensor(out=dt[:], in0=db[:], in1=pr["dtb"][:, d * 8:(d + 1) * 8], op=ALU.add),
                 [db, pr["dtb"]], [dt])
            k.op("act", lambda e: e.activation(out=dt[:], in_=dt[:], func=AF.Exp), [dt], [dt])
            k.op("act", lambda e: e.activation(out=dt[:], in_=dt[:], func=AF.Ln, bias=1.0, scale=1.0), [dt], [dt])
            k.op("dve", lambda e, d=d: e.tensor_tensor(out=a[:], in0=dt[:], in1=A[:, d * 8:(d + 1) * 8], op=ALU.mult), [dt, A], [a])
            k.op("pe", lambda e, tri=tri, neg=neg: e.matmul(ps_m[:, 256:264], lhsT=tri[:], rhs=a[:], start=True, stop=True), [tri, a], [ps_m])
            k.op("pe", lambda e: e.matmul(ps_m[:, 264:272], lhsT=ones[:], rhs=a[:], start=True, stop=True), [ones, a], [ps_m])
            k.op("dve", lambda e: e.tensor_scalar(out=nacs[:], in0=ps_m[:, 256:264], scalar1=-1.0, scalar2=0.0, op0=ALU.mult, op1=ALU.add),
                 [ps_m], [nacs])
            k.op("act", lambda e: e.activation(out=ea[:], in_=ps_m[:, 256:264], func=AF.Exp), [ps_m], [ea])
            k.op("dve", lambda e: e.tensor_tensor(out=dte[:], in0=ps_m[:, 264:272], in1=nacs[:], op=ALU.add), [ps_m, nacs], [dte])
            k.op("act", lambda e: e.activation(out=dte[:], in_=dte[:], func=AF.Exp), [dte], [dte])
            k.op("act", lambda e: e.activation(out=eend[:], in_=ps_m[:, 264:272], func=AF.Exp), [ps_m], [eend])
            for cc in range(4):
                k.op("pe", lambda e, cc=cc, ts=ts: e.transpose(out=ps_x[:, cc * 128:(cc + 1) * 128], in_=xsT[:, cc, ts],
                                                               identity=ident[:]), [xsTb, ident], [ps_x])
            copy_op(k, "act", xst[:], ps_x[:], [ps_x], [xst])
            x3 = xst[:].rearrange("p (h d) -> p h d", h=8)
            k.op("dve", lambda e, x3=x3: e.tensor_tensor(out=xdt[:].rearrange("p (h d) -> p h d", h=8), in0=x3,
                                                         in1=dt[:].unsqueeze(2).to_broadcast([128, 8, 64]), op=ALU.mult),
                 [xst, dt], [xdt])
            k.op("dve", lambda e: e.tensor_tensor(out=xw[:].rearrange("p (h d) -> p h d", h=8),
                                                  in0=xdt[:].rearrange("p (h d) -> p h d", h=8),
                                                  in1=dte[:].unsqueeze(2).to_broadcast([128, 8, 64]), op=ALU.mult),
                 [xdt, dte], [xw])
            for gi in range(2):
                k.op("pe", lambda e, gi=gi, ts=ts: e.transpose(out=ps_bt[:, gi * 128:(gi + 1) * 128], in_=bcT[:, gi, ts],
                                                               identity=identb[:]), [bcTb, identb], [ps_bt])
            copy_op(k, "dve", btok[:], ps_bt[:, 0:256], [ps_bt], [btok])
            for gi in range(2):
                k.op("pe", lambda e, gi=gi, ts=ts: e.matmul(ps_m[:, gi * 128:(gi + 1) * 128], lhsT=bcT[:, gi, ts], rhs=bcT[:, 2 + gi, ts],
                                                            start=True, stop=True), [bcTb], [ps_m])
            for q in range(2):
                pl = ps_L[q]
                k.op("pe", lambda e, tri=tri, neg=neg, pl=pl: e.matmul(pl[:], lhsT=ident[:], rhs=neg[:], start=True, stop=False, skip_group_check=True), [ident, neg], [pl])
                for h4 in range(4):
                    h = q * 4 + h4
                    k.op("pe", lambda e, pl=pl, h4=h4, h=h, tri=tri: e.matmul(pl[:, h4 * 128:(h4 + 1) * 128],
                                                                     lhsT=a[:, h:h + 1].to_broadcast([128, 128]), rhs=tri[:],
                                                                     start=False, stop=True, skip_group_check=True), [a, tri], [pl])
                for h4 in range(4):
                    h = q * 4 + h4
                    k.op("act", lambda e, pl=pl, h4=h4, h=h: e.activation(out=Dm[:, h * 128:(h + 1) * 128],
                                                                          in_=pl[:, h4 * 128:(h4 + 1) * 128], func=AF.Exp,
                                                                          bias=nacs[:, h:h + 1], scale=1.0), [pl, nacs], [Dm])
                k.op("dve", lambda e, q=q: e.tensor_tensor(
                    out=M[:, q * 512:(q + 1) * 512].rearrange("p (h t) -> p h t", h=4),
                    in0=Dm[:, q * 512:(q + 1) * 512].rearrange("p (h t) -> p h t", h=4),
                    in1=ps_m[:, q * 128:(q + 1) * 128].unsqueeze(1).to_broadcast([128, 4, 128]), op=ALU.mult), [Dm, ps_m], [M])
            for h in range(8):
                k.op("pe", lambda e, h=h: e.matmul(ps_y[:, h * 64:(h + 1) * 64], lhsT=M[:, h * 128:(h + 1) * 128],
                                                   rhs=xdt[:, h * 64:(h + 1) * 64], start=True, stop=True), [M, xdt], [ps_y])
            for gi in range(2):
                k.op("pe", lambda e, gi=gi, ts=ts: e.matmul(ps_f[:, gi * 256:(gi + 1) * 256], lhsT=bcT[:, 2 + gi, ts],
                                                            rhs=hbf[:, gi * 256:(gi + 1) * 256], start=True, stop=True),
                     [bcTb, hbf], [ps_f])
            k.op("dve", lambda e: e.tensor_tensor(out=y[:].rearrange("p (h d) -> p h d", h=8),
                                                  in0=ps_f[:].rearrange("p (h d) -> p h d", h=8),
                                                  in1=ea[:].unsqueeze(2).to_broadcast([128, 8, 64]), op=ALU.mult), [ps_f, ea], [y])
            k.op("dve", lambda e: e.tensor_tensor(out=y[:], in0=y[:], in1=ps_y[:], op=ALU.add), [y, ps_y], [y])
            for gi in range(2):
                k.op("pe", lambda e, gi=gi: e.matmul(ps_h[:, gi * 256:(gi + 1) * 256], lhsT=btok[:, gi * 128:(gi + 1) * 128],
                                                     rhs=xw[:, gi * 256:(gi + 1) * 256], start=True, stop=True), [btok, xw], [ps_h])
            k.op("dve", lambda e: e.tensor_tensor(out=hst[:].rearrange("p (h d) -> p h d", h=8),
                                                  in0=hst[:].rearrange("p (h d) -> p h d", h=8),
                                                  in1=eend[:].unsqueeze(2).to_broadcast([128, 8, 64]), op=ALU.mult), [hst, eend], [hst])
            k.op("dve", lambda e: e.tensor_tensor(out=hst[:], in0=hst[:], in1=ps_h[:], op=ALU.add), [hst, ps_h], [hst])
            copy_op(k, "act", hbf[:], hst[:], [hst], [hbf])
            if d == 0:
                k.dma("pool", g.YF[rows, :], y[:], y, reads=[y], writes=[yfd[i]])
            else:
                k.dma("sp", yf[:], g.YF[rows, :], yf, reads=[yfd[i]], writes=[yf])
                k.dma("sp", zt[:], g.H[rows, S_Z:S_Z + 512], zt, writes=[zt])
                k.op("dve", lambda e: e.tensor_tensor(out=y[:], in0=y[:], in1=yf[:], op=ALU.add), [y, yf], [y])
                k.op("dve", lambda e, x3=x3: e.tensor_tensor(out=sq[:].rearrange("p (h d) -> p h d", h=8), in0=x3,
                                                             in1=pr["dsk"][:].unsqueeze(2).to_broadcast([128, 8, 64]), op=ALU.mult),
                     [xst, pr["dsk"]], [sq])
                k.op("dve", lambda e: e.tensor_tensor(out=y[:], in0=y[:], in1=sq[:], op=ALU.add), [y, sq], [y])
                silu_op(k, sz, sz[:], zt, zt[:], sq, sq[:])
                k.op("dve", lambda e: e.tensor_tensor(out=y[:], in0=y[:], in1=sz[:], op=ALU.mult), [y, sz], [y])
                k.op("dve", lambda e: e.tensor_tensor(out=sq[:], in0=y[:], in1=y[:], op=ALU.mult), [y], [sq])
                k.op("dve", lambda e: e.tensor_reduce(out=ssq[:, 0:1], in_=sq[:], axis=AX.X, op=ALU.add), [sq], [ssq])
                rstd_op(k, ssq, ssq[:, 0:1], ssq, ssq[:, 0:1], 1.0 / 512, 1e-6)
                k.op("dve", lambda e: e.scalar_tensor_tensor(out=y[:], in0=y[:], scalar=ssq[:, 0:1], in1=pr["snorm"][:],
                                                             op0=ALU.mult, op1=ALU.mult), [y, ssq, pr["snorm"]], [y])
                for cc in range(4):
                    k.op("pe", lambda e, cc=cc: e.transpose(out=ps_x[:, cc * 128:(cc + 1) * 128], in_=y[:, cc * 128:(cc + 1) * 128],
                                                            identity=ident[:]), [y, ident], [ps_x])
                copy_op(k, "act", yb[:], ps_x[:], [ps_x], [yb])
                k.dma("pool", g.MT[8:12, :, i * 128:(i + 1) * 128].rearrange("c p t -> p c t"),
                      yb[:].rearrange("p (c t) -> p c t", c=4), yb, reads=[yb])
    k.end()


def phase_out(k, g, l, x_ap, y_ap):
    L, NT = g.L, g.NT
    k.begin()
    pr = load_prm(k, g, l, ["lng", "lnb"])
    wo = k.sb("wo", [128, 16, 1024], BF16)
    wf = [k.sb("wf%d" % i, [128, 4, 1024]) for i in range(2)]
    wov = g.w_out[l].rearrange("(c p) n -> p c n", p=128)
    for q in range(4):
        w = wf[q % 2]
        k.dma("sp", w[:], wov[:, q * 4:(q + 1) * 4, :], w, writes=[w])
        copy_op(k, _alt(q), wo[:, q * 4:(q + 1) * 4, :], w[:], [w], [wo])
    mt = [k.sb("mt%d" % i, [128, 16, 128], BF16) for i in range(2)]
    xt = [k.sb("xt%d" % i, [128, 1024]) for i in range(2)]
    r = k.sb("r", [128, 1024]); sq = k.sb("sq", [128, 1024]); st = k.sb("st", [128, 4])
    ot = [k.sb("ot%d" % i, [128, 1024]) for i in range(2)]
    ps = [k.ps("ps%d" % i, [128, 512]) for i in range(4)]
    for i in range(NT):
        m, x, o = mt[i % 2], xt[i % 2], ot[i % 2]
        rows = slice(i * 128, (i + 1) * 128)
        k.dma("sp", m[:], g.MT[:, :, i * 128:(i + 1) * 128].rearrange("c p t -> p c t"), m, writes=[m])
        k.dma("sp", x[:], x_ap[rows, :], x, writes=[x])
        for hf in range(2):
            p = ps[(i % 2) * 2 + hf]
            for cc in range(16):
                k.op("pe", lambda e, p=p, cc=cc, hf=hf, m=m: e.matmul(p[:], lhsT=m[:, cc, :], rhs=wo[:, cc, hf * 512:(hf + 1) * 512],
                                                                      start=(cc == 0), stop=(cc == 15)), [m, wo], [p])
            k.op("dve", lambda e, p=p, hf=hf, x=x: e.scalar_tensor_tensor(
                out=r[:, hf * 512:(hf + 1) * 512], in0=x[:, hf * 512:(hf + 1) * 512], scalar=ALPHA, in1=p[:],
                op0=ALU.mult, op1=ALU.add), [x, p], [r])
        k.op("dve", lambda e: e.tensor_reduce(out=st[:, 0:1], in_=r[:], axis=AX.X, op=ALU.add), [r], [st])
        k.op("dve", lambda e: e.tensor_scalar(out=st[:, 1:2], in0=st[:, 0:1], scalar1=-1.0 / 1024, scalar2=0.0, op0=ALU.mult, op1=ALU.add),
             [st], [st])
        k.op("act", lambda e: e.activation(out=r[:], in_=r[:], func=AF.Identity, bias=st[:, 1:2], scale=1.0), [r, st], [r])
        k.op("pool", lambda e: e.tensor_tensor(out=sq[:], in0=r[:], in1=r[:], op=ALU.mult), [r], [sq])
        k.op("dve", lambda e: e.tensor_reduce(out=st[:, 2:3], in_=sq[:], axis=AX.X, op=ALU.add), [sq], [st])
        rstd_op(k, st, st[:, 2:3], st, st[:, 2:3], 1.0 / 1024, 1e-5)
        k.op("dve", lambda e, o=o: e.scalar_tensor_tensor(out=o[:], in0=r[:], scalar=st[:, 2:3], in1=pr["lng"][:], op0=ALU.mult,
                                                          op1=ALU.mult), [r, st, pr["lng"]], [o])
        k.op("pool", lambda e, o=o: e.tensor_tensor(out=o[:], in0=o[:], in1=pr["lnb"][:], op=ALU.add), [o, pr["lnb"]], [o])
        k.dma("pool", y_ap[rows, :], o[:], o, reads=[o])
    k.end()


def build_nc(L, layers=(0, 1), phases="PAHSGO", debug=False):
    nc = bass.Bass("TRN2", target_bir_lowering=False)
    g = G()
    g.L, g.NT = L, L // 128
    kin = "ExternalInput"
    g.x = nc.dram_tensor("x", [L, D], F32, kind=kin).ap()
    g.w_in = nc.dram_tensor("w_in", [DEPTH, D, NIN], F32, kind=kin).ap()
    g.w_out = nc.dram_tensor("w_out", [DEPTH, 2 * D, D], F32, kind=kin).ap()
    g.cst = nc.dram_tensor("cst", [128, NCST], F32, kind=kin).ap()
    g.rope = nc.dram_tensor("rope", [L, 128], F32, kind=kin).ap()
    g.prm = nc.dram_tensor("prm", [DEPTH, 128, NPRM], F32, kind=kin).ap()
    g.y = nc.dram_tensor("y", [L, D], F32, kind="ExternalOutput").ap()
    dk = "ExternalOutput" if debug else "Internal"
    g.H = nc.dram_tensor("H", [L, NIN], F32, kind=dk).ap()
    g.XBCT = nc.dram_tensor("XBCT", [1024, L], F32, kind=dk).ap()
    g.MT = nc.dram_tensor("MT", [16, 128, L], BF16, kind=dk).ap()
    g.OFH = nc.dram_tensor("OFH", [4, 128, L], F32).ap()
    g.OFG = nc.dram_tensor("OFG", [4, 128, L], F32).ap()
    g.OFH2 = nc.dram_tensor("OFH2", [4, 128, L], F32).ap()
    g.OFG2 = nc.dram_tensor("OFG2", [4, 128, L], F32).ap()
    g.YF = nc.dram_tensor("YF", [L, 512], F32).ap()
    g.X1 = nc.dram_tensor("X1", [L, D], F32, kind=dk).ap()
    k = K(nc)
    for l in layers:
        xin = g.x if l == layers[0] else g.X1
        yout = g.X1 if l != layers[-1] else g.y
        if "P" in phases:
            phase_proj(k, g, l, xin)
        if "A" in phases:
            phase_attn(k, g, l)
        if "H" in phases:
            phase_scan(k, g, l, "hgrn")
        if "S" in phases:
            phase_ssd(k, g, l)
        if "G" in phases:
            phase_scan(k, g, l, "gla")
        if "O" in phases:
            phase_out(k, g, l, xin, yout)
    k.emit()
    g.n_instr = k.n_instr
    return nc, g


_NC_CACHE = {}


def kernel(**inp):
    x = np.asarray(inp["x"], np.float32)
    B, L, _ = x.shape
    if L not in _NC_CACHE:
        _NC_CACHE[L] = build_nc(L)[0]
    nc = _NC_CACHE[L]
    cst, rope = make_consts(L)
    prm = make_params({k_: np.asarray(v, np.float32) for k_, v in inp.items()})
    w_in = np.ascontiguousarray(inp["w_in"], np.float32)
    w_out = np.ascontiguousarray(inp["w_out"], np.float32)
    n = 8
    in_maps = []
    for c in range(n):
        in_maps.append({"x": np.ascontiguousarray(x[c % B]), "w_in": w_in, "w_out": w_out, "cst": cst, "rope": rope,
                        "prm": prm})
    res = run_bass_kernel_spmd(nc, in_maps, core_ids=list(range(n)))
    return np.stack([np.asarray(res.results[b]["y"], np.float32) for b in range(B)], axis=0)
```
